# Optimizing a Trainium2 kernel written in Bass

```python
import math
import jax, jax.numpy as jnp
from jax import lax
import numpy as np

D_MODEL = 2048
BATCH = 2
SEQ = 8192
DEPTH = 1

HEAD_DIM = 128
SB_HEADS = 8
MOBA_HEADS = 8
SB_WIDTH = SB_HEADS * HEAD_DIM
MOBA_WIDTH = MOBA_HEADS * HEAD_DIM
Q_BLOCK = 128
MOBA_BLOCK = 256
MOBA_TOPK = 3
MOBA_Q_CHUNK = 32
REL_BUCKETS = 32
REL_MAX_DIST = 1024
PEER_HEADS = 8
PEER_NKEYS = 128
PEER_N_EXPERTS = PEER_NKEYS * PEER_NKEYS
PEER_QDIM = 256
PEER_TOPK = 16
PEER_TOK_CHUNK = 128
LN_EPS = 1e-5
DN_ALPHA = (2.0 * DEPTH) ** 0.25
DN_BETA = (8.0 * DEPTH) ** -0.25
NEG = -1e30

kernel_name = "hybrid_sb_moba_peer_deepnorm"


def layer_norm(x, g, b):
    xf = x.astype(jnp.float32)
    mu = jnp.mean(xf, axis=-1, keepdims=True)
    var = jnp.mean(jnp.square(xf - mu), axis=-1, keepdims=True)
    y = (xf - mu) * lax.rsqrt(var + LN_EPS)
    return (y * g.astype(jnp.float32) + b.astype(jnp.float32)).astype(x.dtype)


def rel_bucket(rel):
    n_exact = REL_BUCKETS // 2
    rel = jnp.maximum(rel, 0)
    logd = jnp.log(jnp.maximum(rel, 1).astype(jnp.float32) / n_exact) / math.log(REL_MAX_DIST / n_exact)
    large = n_exact + (logd * (REL_BUCKETS - n_exact)).astype(jnp.int32)
    large = jnp.minimum(large, REL_BUCKETS - 1)
    return jnp.where(rel < n_exact, rel, large)


def stick_breaking_attention(q, k, v):
    B, H, S, dh = q.shape
    nblk = S // Q_BLOCK
    scale = dh ** -0.5
    qb = q.reshape(B, H, nblk, Q_BLOCK, dh).transpose(2, 0, 1, 3, 4)
    key_pos = jnp.arange(S)

    def block(args):
        qi, i = args
        z = jnp.einsum('bhqd,bhkd->bhqk', qi, k, preferred_element_type=jnp.float32) * scale
        q_pos = i * Q_BLOCK + jnp.arange(Q_BLOCK)
        causal = key_pos[None, :] < q_pos[:, None]
        log_1mb = jnp.where(causal, jax.nn.log_sigmoid(-z), 0.0)
        tail = lax.cumsum(log_1mb, axis=3, reverse=True) - log_1mb
        w = jnp.where(causal, jnp.exp(jax.nn.log_sigmoid(z) + tail), 0.0)
        return jnp.einsum('bhqk,bhkd->bhqd', w.astype(v.dtype), v)

    out = lax.map(block, (qb, jnp.arange(nblk)))
    return out.transpose(1, 2, 0, 3, 4).reshape(B, H, S, dh)


def moba_attention(q, k, v, rel_bias):
    B, H, S, dh = q.shape
    nb = S // MOBA_BLOCK
    ksel = min(MOBA_TOPK, nb)
    nchunk = S // MOBA_Q_CHUNK
    scale = dh ** -0.5
    kb = k.reshape(B, H, nb, MOBA_BLOCK, dh)
    vb = v.reshape(B, H, nb, MOBA_BLOCK, dh)
    k_mean = jnp.mean(kb.astype(jnp.float32), axis=3)
    qc = q.reshape(B, H, nchunk, MOBA_Q_CHUNK, dh).transpose(2, 0, 1, 3, 4)
    b_idx = jnp.arange(B)[:, None, None, None]
    h_idx = jnp.arange(H)[None, :, None, None]
    h5 = jnp.arange(H)[None, :, None, None, None]
    offs = jnp.arange(MOBA_BLOCK)
    blk_ids = jnp.arange(nb)

    def chunk(args):
        qi, c = args
        q_pos = c * MOBA_Q_CHUNK + jnp.arange(MOBA_Q_CHUNK)
        own = (c * MOBA_Q_CHUNK) // MOBA_BLOCK
        gate = jnp.einsum('bhqd,bhnd->bhqn', qi.astype(jnp.float32), k_mean)
        gate = jnp.where(blk_ids < own, gate, NEG)
        _, sel = lax.top_k(gate, ksel)
        sel_valid = jnp.arange(ksel) < own
        sel = jnp.where(sel_valid, sel, 0)
        k_sel = kb[b_idx, h_idx, sel]
        v_sel = vb[b_idx, h_idx, sel]
        s_sel = jnp.einsum('bhqd,bhqrkd->bhqrk', qi, k_sel, preferred_element_type=jnp.float32) * scale
        pos_sel = sel[..., None] * MOBA_BLOCK + offs
        bucket_sel = rel_bucket(q_pos[None, None, :, None, None] - pos_sel)
        s_sel = s_sel + rel_bias[h5, bucket_sel].astype(jnp.float32)
        s_sel = jnp.where(sel_valid[:, None], s_sel, NEG)
        k_own = lax.dynamic_index_in_dim(kb, own, axis=2, keepdims=False)
        v_own = lax.dynamic_index_in_dim(vb, own, axis=2, keepdims=False)
        own_pos = own * MOBA_BLOCK + offs
        s_own = jnp.einsum('bhqd,bhkd->bhqk', qi, k_own, preferred_element_type=jnp.float32) * scale
        s_own = s_own + rel_bias[:, rel_bucket(q_pos[:, None] - own_pos[None, :])][None].astype(jnp.float32)
        s_own = jnp.where(own_pos[None, :] <= q_pos[:, None], s_own, NEG)
        Qc = MOBA_Q_CHUNK
        logits = jnp.concatenate([s_sel.reshape(B, H, Qc, ksel * MOBA_BLOCK), s_own], axis=-1)
        p = jax.nn.softmax(logits, axis=-1).astype(v.dtype)
        p_sel = p[..., :ksel * MOBA_BLOCK].reshape(B, H, Qc, ksel, MOBA_BLOCK)
        p_own = p[..., ksel * MOBA_BLOCK:]
        return (jnp.einsum('bhqrk,bhqrkd->bhqd', p_sel, v_sel)
                + jnp.einsum('bhqk,bhkd->bhqd', p_own, v_own))

    out = lax.map(chunk, (qc, jnp.arange(nchunk)))
    return out.transpose(1, 2, 0, 3, 4).reshape(B, H, S, dh)


def token_mixer(x, w_in, w_gate, b_gate, w_branch_sb, w_branch_moba, w_out, rel_bias):
    B, S, D = x.shape
    proj = x @ w_in
    cuts = np.cumsum([SB_WIDTH, SB_WIDTH, SB_WIDTH, MOBA_WIDTH, MOBA_WIDTH])
    q_sb, k_sb, v_sb, q_mb, k_mb, v_mb = jnp.split(proj, cuts, axis=-1)

    def heads(t, h):
        return t.reshape(B, S, h, HEAD_DIM).transpose(0, 2, 1, 3)

    def merge(t):
        return t.transpose(0, 2, 1, 3).reshape(B, S, -1)

    o_sb = stick_breaking_attention(heads(q_sb, SB_HEADS), heads(k_sb, SB_HEADS), heads(v_sb, SB_HEADS))
    o_mb = moba_attention(heads(q_mb, MOBA_HEADS), heads(k_mb, MOBA_HEADS), heads(v_mb, MOBA_HEADS), rel_bias)
    y_sb = merge(o_sb) @ w_branch_sb
    y_mb = merge(o_mb) @ w_branch_moba
    gates = jax.nn.sigmoid((x @ w_gate + b_gate).astype(jnp.float32)).astype(x.dtype)
    g_sb, g_mb = jnp.split(gates, 2, axis=-1)
    return (g_sb * y_sb + g_mb * y_mb) @ w_out


def peer_ffn(x, w_peer_query, peer_sub_keys, peer_u, peer_v):
    B, S, D = x.shape
    tokens = x.reshape(-1, D)
    nchunk = tokens.shape[0] // PEER_TOK_CHUNK
    Tc, H, K = PEER_TOK_CHUNK, PEER_HEADS, PEER_TOPK

    def chunk(xc):
        q = (xc @ w_peer_query).reshape(Tc, H, 2, PEER_QDIM // 2)
        s = jnp.einsum('thpd,pnd->thpn', q, peer_sub_keys, preferred_element_type=jnp.float32)
        top_s, top_i = lax.top_k(s, K)
        cand_s = top_s[:, :, 0, :, None] + top_s[:, :, 1, None, :]
        cand_i = top_i[:, :, 0, :, None] * PEER_NKEYS + top_i[:, :, 1, None, :]
        best_s, best_pos = lax.top_k(cand_s.reshape(Tc, H, K * K), K)
        expert = jnp.take_along_axis(cand_i.reshape(Tc, H, K * K), best_pos, axis=-1)
        g = jax.nn.softmax(best_s, axis=-1)
        u = peer_u[expert]
        v = peer_v[expert]
        act = jax.nn.gelu(jnp.einsum('td,thkd->thk', xc, u, preferred_element_type=jnp.float32),
                          approximate=False)
        return jnp.einsum('thk,thkd->td', (g * act).astype(v.dtype), v)

    out = lax.map(chunk, tokens.reshape(nchunk, Tc, D))
    return out.reshape(B, S, D)


def setup_inputs(seed: int = 0) -> dict:
    key = jax.random.key(seed)
    ks = jax.random.split(key, 20)
    D = D_MODEL
    f = jnp.float32
    sd = D ** -0.5
    x = jax.random.normal(ks[0], (BATCH, SEQ, D), f)
    w_qk_sb = jax.random.normal(ks[1], (D, 2 * SB_WIDTH), f) * sd
    w_v_sb = jax.random.normal(ks[2], (D, SB_WIDTH), f) * sd * DN_BETA
    w_qk_mb = jax.random.normal(ks[3], (D, 2 * MOBA_WIDTH), f) * sd
    w_v_mb = jax.random.normal(ks[4], (D, MOBA_WIDTH), f) * sd * DN_BETA
    w_in = jnp.concatenate([w_qk_sb, w_v_sb, w_qk_mb, w_v_mb], axis=1)
    w_gate = jax.random.normal(ks[5], (D, 2 * D), f) * sd
    b_gate = jax.random.normal(ks[6], (2 * D,), f) * 0.01
    w_branch_sb = jax.random.normal(ks[7], (SB_WIDTH, D), f) * SB_WIDTH ** -0.5 * DN_BETA
    w_branch_moba = jax.random.normal(ks[8], (MOBA_WIDTH, D), f) * MOBA_WIDTH ** -0.5 * DN_BETA
    w_out = jax.random.normal(ks[9], (D, D), f) * sd * DN_BETA
    rel_bias = jax.random.normal(ks[10], (MOBA_HEADS, REL_BUCKETS), f) * 0.2
    ln1_g = 1.0 + 0.01 * jax.random.normal(ks[11], (D,), f)
    ln1_b = 0.01 * jax.random.normal(ks[12], (D,), f)
    w_peer_query = jax.random.normal(ks[13], (D, PEER_HEADS * PEER_QDIM), f) * sd
    peer_sub_keys = jax.random.normal(ks[14], (2, PEER_NKEYS, PEER_QDIM // 2), f) * (PEER_QDIM // 2) ** -0.5
    peer_u = jax.random.normal(ks[15], (PEER_N_EXPERTS, D), f) * sd
    peer_v = jax.random.normal(ks[16], (PEER_N_EXPERTS, D), f) * (PEER_HEADS * PEER_TOPK) ** -0.5 * DN_BETA
    ln2_g = 1.0 + 0.01 * jax.random.normal(ks[17], (D,), f)
    ln2_b = 0.01 * jax.random.normal(ks[18], (D,), f)
    return {"x": x, "w_in": w_in, "w_gate": w_gate, "b_gate": b_gate,
            "w_branch_sb": w_branch_sb, "w_branch_moba": w_branch_moba, "w_out": w_out,
            "rel_bias": rel_bias, "ln1_g": ln1_g, "ln1_b": ln1_b,
            "w_peer_query": w_peer_query, "peer_sub_keys": peer_sub_keys,
            "peer_u": peer_u, "peer_v": peer_v, "ln2_g": ln2_g, "ln2_b": ln2_b}


def reference(x, w_in, w_gate, b_gate, w_branch_sb, w_branch_moba, w_out, rel_bias,
              ln1_g, ln1_b, w_peer_query, peer_sub_keys, peer_u, peer_v, ln2_g, ln2_b):
    h = x
    for _ in range(DEPTH):
        mix = token_mixer(h, w_in, w_gate, b_gate, w_branch_sb, w_branch_moba, w_out, rel_bias)
        h = layer_norm(DN_ALPHA * h + mix, ln1_g, ln1_b)
        ffn = peer_ffn(h, w_peer_query, peer_sub_keys, peer_u, peer_v)
        h = layer_norm(DN_ALPHA * h + ffn, ln2_g, ln2_b)
    return h
```

```python
import numpy as np
import concourse.bass as bass
import concourse.mybir as mybir
from concourse.bass_utils import run_bass_kernel_spmd
from concourse.bass_types import AP

F32 = mybir.dt.float32
BF16 = mybir.dt.bfloat16
U32 = mybir.dt.uint32
AF = mybir.ActivationFunctionType
ALU = mybir.AluOpType
AX = mybir.AxisListType

ENGS = ["pe", "act", "dve", "pool", "sp"]
D = 2048
S = 8192
NT = 2048
ALPHA = 2.0 ** 0.25
EPS = 1e-5
NEGB = -30000.0
LBV = 2176


class Prog:
    def __init__(self, nc):
        self.nc = nc
        self.ops = {e: [] for e in ENGS}
        self.cnt = {e: 0 for e in ENGS}
        self.sems = {}
        self.dma_cnt = {}
        self.last_w = {}
        self.readers = {}
        self.waited = {e: {} for e in ENGS}
        self._stack = []

    def sem(self, key):
        if key not in self.sems:
            cm = self.nc.semaphore("s_" + key)
            self.sems[key] = cm.__enter__()
            self._stack.append(cm)
        return self.sems[key]

    def _deps(self, eng, reads, writes):
        deps = {}

        def add(d):
            if d is not None and deps.get(d[0], 0) < d[1]:
                deps[d[0]] = d[1]
        for b in reads:
            add(self.last_w.get(b))
        for b in writes:
            add(self.last_w.get(b))
            for r in self.readers.get(b, []):
                add(r)
        waits = []
        for k, v in deps.items():
            if k == "e_pe" and eng == "pe":
                continue
            if self.waited[eng].get(k, 0) >= v:
                continue
            self.waited[eng][k] = v
            self.sem(k)
            waits.append((k, v))
        return waits

    def _commit(self, reads, writes, me):
        for b in reads:
            self.readers.setdefault(b, []).append(me)
        for b in writes:
            self.last_w[b] = me
            self.readers[b] = []

    def op(self, eng, fn, reads=(), writes=()):
        waits = self._deps(eng, reads, writes)
        self.cnt[eng] += 1
        me = ("e_" + eng, self.cnt[eng])
        self.sem(me[0])
        self.ops[eng].append((fn, waits, (me[0], 1)))
        self._commit(reads, writes, me)

    def dma(self, eng, fn, reads=(), writes=(), chan=None):
        if chan is None:
            chan = (list(writes) + list(reads))[0]
        waits = self._deps(eng, reads, writes)
        key = "d_" + chan
        self.dma_cnt[key] = self.dma_cnt.get(key, 0) + 16
        me = (key, self.dma_cnt[key])
        self.sem(key)
        self.ops[eng].append((fn, waits, (key, 16)))
        self._commit(reads, writes, me)

    def barrier(self, engs=ENGS):
        waits = [("e_" + e, self.cnt[e]) for e in ENGS if self.cnt[e] > 0]
        waits += list(self.dma_cnt.items())
        for e in engs:
            w = [(k, v) for k, v in waits if self.waited[e].get(k, 0) < v and k != "e_" + e]
            for k, v in w:
                self.waited[e][k] = v
            self.ops[e].append((None, w, None))
        if len(engs) == len(ENGS):
            self.last_w = {}
            self.readers = {}

    def emit(self):
        nc = self.nc
        P = self
        with nc.Block() as block:
            def run(e, name):
                for fn, waits, inc in P.ops[name]:
                    for k, v in waits:
                        e.wait_ge(P.sems[k], v)
                    if fn is not None:
                        fn(e).then_inc(P.sems[inc[0]], inc[1])

            @block.sync
            def _(e):
                run(e, "sp")

            @block.scalar
            def _(e):
                run(e, "act")

            @block.vector
            def _(e):
                run(e, "dve")

            @block.gpsimd
            def _(e):
                run(e, "pool")

            @block.tensor
            def _(e):
                run(e, "pe")


def C(name, *a, **kw):
    return lambda e: getattr(e, name)(*a, **kw)


class Alloc:
    def __init__(self, nc):
        self.nc = nc
        self._stack = []

    def sb(self, name, shape, dt):
        cm = self.nc.sbuf_tensor("sb_" + name, shape, dt)
        t = cm.__enter__()
        self._stack.append(cm)
        return t

    def ps(self, name, shape, dt=F32):
        cm = self.nc.psum_tensor("ps_" + name, shape, dt)
        t = cm.__enter__()
        self._stack.append(cm)
        return t

    def close(self):
        for cm in reversed(self._stack):
            cm.__exit__(None, None, None)
        self._stack = []


def build(debug=False):
    nc = bass.Bass("TRN2", target_bir_lowering=False)

    def din(name, shape, dt=F32):
        return nc.dram_tensor(name, list(shape), dt, kind="ExternalInput").ap()

    def dscr(name, shape, dt):
        return nc.dram_tensor(name, list(shape), dt, kind="Internal").ap()

    xT_b = din("xT_b", [D, S])
    x_own = din("x_own", [NT, D])
    xT_own = din("xT_own", [D, NT])
    wq = din("wq", [D, 2048]); wk = din("wk", [D, 2048]); wv = din("wv", [D, 2048])
    w_gate = din("w_gate", [D, 4096]); bgc_d = din("bgc", [128, 32])
    wbs = din("wbs", [1024, D]); wbm = din("wbm", [1024, D]); w_out = din("w_out", [D, D])
    rbT_d = din("rbT", [32, 8]); rb31_d = din("rb31", [1, 8]); ohx_d = din("ohx", [32, LBV])
    ln_d = din("lnp", [4, D])
    wpqT = din("wpqT", [2048, D]); keysT_d = din("keysT", [128, 256])
    upT = din("upT", [D, 16384]); vp = din("vp", [16384, D])
    qpos_d = din("qpos", [1, NT]); qblkc_d = din("qblkc", [128, 16])
    cm_d = din("cmat", [128, 512]); esel_d = din("esel", [32, 32 * 128])
    misc_d = din("misc", [128, 176])
    out_d = nc.dram_tensor("out", [NT, D], F32, kind="ExternalOutput").ap()
    if debug:
        dbg_o = nc.dram_tensor("dbg_o", [16, 128, NT], BF16, kind="ExternalOutput").ap()
        dbg_x1 = nc.dram_tensor("dbg_x1", [NT, D], F32, kind="ExternalOutput").ap()

    kTs = dscr("kTs", [16, 128, S], BF16)
    vS = dscr("vS", [16, 128, 64, 128], BF16)
    qTs = dscr("qTs", [16, 128, NT], BF16)
    oTs = dscr("oTs", [16, 128, NT], BF16)
    bvs = dscr("bvs", [8, LBV], F32)
    x1S = dscr("x1S", [16, 128, D], F32)
    x1Ts = dscr("x1Ts", [16, 128, 16, 128], BF16)
    Wd = dscr("Wd", [16, 2, 128, 128, 64], BF16)

    P = Prog(nc)
    G = Alloc(nc)
    cm = G.sb("cm", [128, 4, 128], BF16)
    identf = G.sb("identf", [128, 128], F32)
    misc = G.sb("misc", [128, 176], F32)
    qposB = G.sb("qposB", [128, 256], F32)
    qblkc = G.sb("qblkc", [128, 16], F32)
    rb31c = G.sb("rb31c", [128, 8], F32)
    nrb31c = G.sb("nrb31c", [128, 8], F32)
    bgc = G.sb("bgc", [128, 32], F32)
    pb = [G.ps(f"pb{i}", [128, 512], F32) for i in range(8)]
    ident = cm[:, 0, :]; ones = cm[:, 1, :]; negones = cm[:, 2, :]; negtri = cm[:, 3, :]
    iota_n = misc[:, 0:32]

    P.dma("pool", C("dma_start", out=cm[:].rearrange("p a b -> p (a b)"), in_=cm_d), writes=["cm"])
    P.dma("sp", C("dma_start", out=identf[:], in_=cm_d[:, 0:128]), writes=["identf"])
    P.dma("sp", C("dma_start", out=misc[:], in_=misc_d), writes=["misc"])
    P.dma("sp", C("dma_start", out=qposB[:], in_=AP(tensor=qpos_d.tensor, offset=0, ap=[[0, 128], [1, 256]])), writes=["qposB"])
    P.dma("sp", C("dma_start", out=qblkc[:], in_=qblkc_d), writes=["qblkc"])
    P.dma("sp", C("dma_start", out=rb31c[:], in_=AP(tensor=rb31_d.tensor, offset=0, ap=[[0, 128], [1, 8]])), writes=["rb31c"])
    P.dma("sp", C("dma_start", out=bgc[:], in_=bgc_d), writes=["bgc"])
    P.op("dve", C("tensor_scalar", out=nrb31c[:], in0=rb31c[:], scalar1=-1.0, scalar2=None, op0=ALU.mult), reads=["rb31c"], writes=["nrb31c"])
    evc = [0]

    def evac(out, in_, reads, writes, scale=None):
        evc[0] += 1
        if evc[0] % 2 == 0:
            if scale is None:
                P.op("act", C("activation", out=out, in_=in_, func=AF.Copy), reads=reads, writes=writes)
            else:
                P.op("act", C("activation", out=out, in_=in_, func=AF.Copy, scale=scale), reads=reads, writes=writes)
        else:
            if scale is None:
                P.op("dve", C("tensor_copy", out=out, in_=in_), reads=reads, writes=writes)
            else:
                P.op("dve", C("tensor_scalar", out=out, in0=in_, scalar1=scale, scalar2=None, op0=ALU.mult), reads=reads, writes=writes)

    A1 = Alloc(nc)
    BWa = A1.sb("BW", [128, 2, 16, 2048], BF16)
    XG = A1.sb("XG", [128, 2, 16, 512], BF16)
    ST = A1.sb("ST", [128, 4, 512], BF16)
    stc = [0]

    def load_w(src, ws):
        for dc in range(16):
            P.dma("pool", C("dma_start", out=BWa[:, ws, dc, :], in_=src[dc * 128:(dc + 1) * 128, :]), writes=[f"BW{ws}"], chan=f"BW{ws}")

    def load_x(src, g, slot):
        for dc in range(16):
            P.dma("pool", C("dma_start", out=XG[:, slot, dc, :], in_=src[dc * 128:(dc + 1) * 128, g * 512:(g + 1) * 512]),
                  writes=[f"XG{slot}"], chan=f"XG{slot}")

    def featmajor_pass(xsrc, ngroups, dst, scale, ws):
        for g in range(ngroups):
            slot = g % 2
            load_x(xsrc, g, slot)
            for fc in range(16):
                ps = pb[fc % 4]
                for dc in range(16):
                    P.op("pe", C("matmul", ps[:], lhsT=BWa[:, ws, dc, fc * 128:(fc + 1) * 128], rhs=XG[:, slot, dc, :], start=(dc == 0), stop=(dc == 15)),
                         reads=[f"BW{ws}", f"XG{slot}"], writes=[f"pb{fc % 4}"])
                s = stc[0] % 4; stc[0] += 1
                evac(ST[:, s, :], ps[:], [f"pb{fc % 4}"], [f"ST{s}"], scale=scale)
                P.dma("sp", C("dma_start", out=dst[fc, :, g * 512:(g + 1) * 512], in_=ST[:, s, :]), reads=[f"ST{s}"], writes=["scr1"], chan=f"ST{s}_st")

    load_w(wk, 0)
    load_w(wv, 1)
    featmajor_pass(xT_b, 16, kTs, None, 0)
    load_w(wq, 0)
    for g in range(16):
        slot = g % 2
        load_x(xT_b, g, slot)
        for tt in range(4):
            for fg in range(4):
                ps = pb[fg]
                for dc in range(16):
                    P.op("pe", C("matmul", ps[:], lhsT=XG[:, slot, dc, tt * 128:(tt + 1) * 128], rhs=BWa[:, 1, dc, fg * 512:(fg + 1) * 512], start=(dc == 0), stop=(dc == 15)),
                         reads=["BW1", f"XG{slot}"], writes=[f"pb{fg}"])
                s = stc[0] % 4; stc[0] += 1
                evac(ST[:, s, :], ps[:], [f"pb{fg}"], [f"ST{s}"])
                tile_i = g * 4 + tt
                P.dma("sp", C("dma_start", out=vS[fg * 4:(fg + 1) * 4, :, tile_i, :].rearrange("h p d -> p h d"), in_=ST[:, s, :].rearrange("p (h d) -> p h d", h=4)),
                      reads=[f"ST{s}"], writes=["scr1"], chan=f"ST{s}_st")
    featmajor_pass(xT_own, 4, qTs, 128.0 ** -0.5, 0)
    P.barrier()
    A1.close()

    A2 = Alloc(nc)
    KTa = A2.sb("KT", [128, 2, S], BF16)
    VHa = A2.sb("VH", [128, 2, 64, 128], BF16)
    QHa = A2.sb("QH", [128, 2, NT], BF16)
    SPall = A2.sb("SPall", [128, 64, 512], BF16)
    Wb = A2.sb("Wb", [128, 6, 512], BF16)
    Cs32 = A2.sb("Cs32", [128, 512], F32)
    CsB = A2.sb("CsB", [128, 2, 512], BF16)
    OT = A2.sb("OT", [128, NT], BF16)
    zer = A2.sb("zer", [128, 512], BF16)
    Hk = A2.sb("Hk", [128, LBV], F32)
    EBs = A2.sb("EBs", [128, 2048], BF16)
    kmf = A2.sb("kmf", [128, 32], F32)
    kmb = A2.sb("kmb", [128, 32], BF16)
    gm = A2.sb("gm", [128, 4, 32], F32)
    g8 = A2.sb("g8", [128, 4, 8], F32)
    t1 = A2.sb("t1", [128, 4, 32], F32)
    selb = A2.sb("selb", [128, 4, 32], BF16)
    selT = A2.sb("selT", [32, 512], BF16)
    rl = A2.sb("rl", [128, 512], F32)
    rbT = A2.sb("rbT", [32, 8], F32)
    ohx = A2.sb("ohx", [32, LBV], F32)
    bvx = Hk[0:8, :]
    esel = A2.sb("esel", [32, 32, 128], BF16)
    masks = A2.sb("masks", [128, 16, 256], BF16)
    blkm = A2.sb("blkm", [128, 3, 16, 32], F32)
    P.dma("pool", C("dma_start", out=esel[:].rearrange("p a b -> p (a b)"), in_=esel_d), writes=["esel"])
    for mm in range(8):
        P.op("dve", C("tensor_scalar", out=masks[:, mm, :], in0=qposB[:], scalar1=misc[:, 32 + mm:33 + mm], scalar2=NEGB, op0=ALU.is_le, op1=ALU.mult),
             reads=["qposB", "misc"], writes=["masks"])
        P.op("dve", C("tensor_scalar", out=masks[:, 8 + mm, :], in0=qposB[:], scalar1=misc[:, 32 + mm:33 + mm], scalar2=NEGB, op0=ALU.is_lt, op1=ALU.mult),
             reads=["qposB", "misc"], writes=["masks"])
    for tt in range(16):
        for j, (opc, val) in enumerate([(ALU.is_lt, NEGB), (ALU.is_gt, NEGB), (ALU.is_ge, -1e30)]):
            P.op("dve", C("tensor_scalar", out=blkm[:, j, tt, :], in0=iota_n, scalar1=qblkc[:, tt:tt + 1], scalar2=val, op0=opc, op1=ALU.mult),
                 reads=["misc", "qblkc"], writes=["blkm"])


    P.dma("sp", C("dma_start", out=rbT[:], in_=rbT_d), writes=["rbT"])
    P.dma("sp", C("dma_start", out=ohx[:], in_=ohx_d), writes=["ohx"])
    for c in range(5):
        n0 = c * 512; n1 = min(LBV, n0 + 512)
        P.op("pe", C("matmul", pb[c][0:8, 0:n1 - n0], lhsT=rbT[:], rhs=ohx[:, n0:n1], start=True, stop=True), reads=["rbT", "ohx"], writes=[f"pb{c}"])
        P.op("dve", C("tensor_copy", out=bvx[:, n0:n1], in_=pb[c][0:8, 0:n1 - n0]), reads=[f"pb{c}"], writes=["Hk"])
    P.dma("sp", C("dma_start", out=bvs, in_=bvx), reads=["Hk"], writes=["bvs"])

    def load_head(hh):
        hs = hh % 2
        P.dma("sp", C("dma_start", out=KTa[:, hs, :], in_=kTs[hh]), writes=[f"KT{hs}"])
        P.dma("sp", C("dma_start", out=VHa[:, hs, :, :].rearrange("p t d -> p (t d)"), in_=vS[hh].rearrange("p t d -> p (t d)")), writes=[f"VH{hs}"])
        P.dma("sp", C("dma_start", out=QHa[:, hs, :], in_=qTs[hh]), writes=[f"QH{hs}"])

    def store_head(hh):
        P.dma("sp", C("dma_start", out=oTs[hh], in_=OT[:]), reads=["OT"], writes=["oTs"], chan="OT_st")
        if debug:
            P.dma("sp", C("dma_start", out=dbg_o[hh], in_=OT[:]), reads=["OT"], writes=["dbg_o"], chan="OT_st2")

    P.op("dve", C("memset", zer[:], 0.0), writes=["zer"])
    def sb_pair(h, kp):
        hs = h % 2
        KT = KTa[:, hs, :]; VH = VHa[:, hs, :, :]; QH = QHa[:, hs, :]
        kKT = f"KT{hs}"; kVH = f"VH{hs}"; kQH = f"QH{hs}"
        k0 = 2 * kp; k1 = k0 + 1
        q = QH[:, k0 * 256:k0 * 256 + 512]
        pob = 6 + (kp % 2)
        po = pb[pob]
        pokey = f"pb{pob}"
        tiles = list(range(8 * k1 + 7, -1, -1))
        nt_ = len(tiles)
        P.op("pool", C("memset", Cs32[:], 0.0), writes=["Cs32"])
        for cs in range(2):
            P.op("dve", C("memset", CsB[:, cs, :], 0.0), writes=[f"CsB{cs}"])
        P.op("pe", C("matmul", po[:], lhsT=VH[:, 0, :], rhs=zer[:], start=True, stop=False), reads=[kVH, "zer"], writes=[pokey])

        def rng(m):
            return 256 if m >= 8 * k1 else 0

        def qk(i, m, zb, more):
            lo = rng(m)
            z = pb[zb]
            diag = m >= 8 * k0
            P.op("pe", C("matmul", z[:, lo:512], lhsT=KT[:, m * 128:(m + 1) * 128], rhs=q[:, lo:512], start=True, stop=not (diag or more)), reads=[kKT, kQH], writes=[f"pb{zb}"])
            if m >= 8 * k1:
                P.op("pe", C("matmul", z[:, 256:512], lhsT=ident, rhs=masks[:, m - 8 * k1, :], start=False, stop=not more), reads=["cm", "masks"], writes=[f"pb{zb}"])
            elif m >= 8 * k0:
                P.op("pe", C("matmul", z[:, 0:256], lhsT=ident, rhs=masks[:, m - 8 * k0, :], start=False, stop=not more), reads=["cm", "masks"], writes=[f"pb{zb}"])

        for i, m in enumerate(tiles):
            zb = i % 5; lo = rng(m)
            qk(i, m, zb, False)
            P.op("act", C("activation", out=pb[zb][:, lo:512], in_=pb[zb][:, lo:512], func=AF.Exp), reads=[f"pb{zb}"], writes=[f"pb{zb}"])
            P.op("act", C("activation", out=SPall[:, i, lo:512], in_=pb[zb][:, lo:512], func=AF.Ln, bias=1.0), reads=[f"pb{zb}"], writes=[f"SPall{i % 8}"])

        def p2a(i, m):
            zb = i % 5; s_ = i % 6; lo = rng(m); cs = i % 2
            z = pb[zb]
            qk(i, m, zb, True)
            P.op("pe", C("matmul", z[:, lo:512], lhsT=negtri, rhs=SPall[:, i, lo:512], start=False, stop=(i == 0)), reads=["cm", f"SPall{i % 8}"], writes=[f"pb{zb}"])
            if i > 0:
                P.op("pe", C("matmul", z[:, lo:512], lhsT=negones, rhs=CsB[:, cs, lo:512], start=False, stop=True), reads=["cm", f"CsB{cs}"], writes=[f"pb{zb}"])
            P.op("act", C("activation", out=Wb[:, s_, lo:512], in_=z[:, lo:512], func=AF.Exp), reads=[f"pb{zb}"], writes=[f"Wb{s_}"])
            if i < nt_ - 1:
                P.op("pool", C("tensor_tensor", out=Cs32[:, lo:512], in0=Cs32[:, lo:512], in1=SPall[:, i, lo:512], op=ALU.add), reads=[f"SPall{i % 8}", "Cs32"], writes=["Cs32"])
                P.op("dve", C("tensor_copy", out=CsB[:, 1 - cs, :], in_=Cs32[:]), reads=["Cs32"], writes=[f"CsB{1 - cs}"])

        def p2b(i, m):
            s_ = i % 6; lo = rng(m)
            P.op("pe", C("matmul", po[:, lo:512], lhsT=VH[:, m, :], rhs=Wb[:, s_, lo:512], start=False, stop=(i == nt_ - 1)), reads=[kVH, f"Wb{s_}"], writes=[pokey])

        for st in range(nt_ + 2):
            if st < nt_:
                p2a(st, tiles[st])
            if 0 <= st - 2 < nt_:
                p2b(st - 2, tiles[st - 2])
        evac(OT[:, k0 * 256:k0 * 256 + 512], po[:], [pokey], ["OT"])

    load_head(0)
    for h in range(8):
        load_head(h + 1)
        for kp in range(4):
            sb_pair(h, kp)
        store_head(h)

    def moba_pair(h, kp):
        hs = (8 + h) % 2
        KT = KTa[:, hs, :]; VH = VHa[:, hs, :, :]; QH = QHa[:, hs, :]
        kKT = f"KT{hs}"; kVH = f"VH{hs}"; kQH = f"QH{hs}"
        k0 = 2 * kp; k1 = k0 + 1
        q = QH[:, k0 * 256:k0 * 256 + 512]
        pg = pb[5]
        for tt in range(4):
            ti = 2 * k0 + tt
            P.op("pe", C("matmul", pg[:, tt * 32:(tt + 1) * 32], lhsT=QH[:, k0 * 256 + tt * 128:k0 * 256 + (tt + 1) * 128], rhs=kmb[:], start=True, stop=True),
                 reads=[kQH, "kmb"], writes=["pb5"])
            P.op("dve", C("tensor_tensor", out=gm[:, tt, :], in0=pg[:, tt * 32:(tt + 1) * 32], in1=blkm[:, 2, ti, :], op=ALU.add), reads=["pb5", "blkm"], writes=["gm"])
            P.op("dve", C("max", out=g8[:, tt, :], in_=gm[:, tt, :]), reads=["gm"], writes=["g8"])
            P.op("dve", C("tensor_scalar", out=t1[:, tt, :], in0=gm[:, tt, :], scalar1=g8[:, tt, 2:3], scalar2=None, op0=ALU.is_lt), reads=["gm", "g8"], writes=["t1"])
            P.op("dve", C("tensor_tensor", out=t1[:, tt, :], in0=t1[:, tt, :], in1=blkm[:, 0, ti, :], op=ALU.mult), reads=["t1", "blkm"], writes=["t1"])
            P.op("dve", C("tensor_tensor", out=selb[:, tt, :], in0=t1[:, tt, :], in1=blkm[:, 1, ti, :], op=ALU.add), reads=["t1", "blkm"], writes=["selb"])
        pgT = pb[5][0:32, 256:512].bitcast(BF16)
        for tt in range(4):
            P.op("pe", C("transpose", pgT[:, tt * 128:(tt + 1) * 128], selb[:, tt, :], ident), reads=["selb", "cm"], writes=["pb5"])
        P.op("dve", C("tensor_copy", out=selT[:], in_=pgT), reads=["pb5"], writes=["selT"])
        po = pb[6]; pl = pb[7]
        ntile = 8 * k1 + 8

        def rng(m):
            return 256 if m >= 8 * k1 else 0

        def mA(m):
            zb = m % 5; s_ = m % 6; lo = rng(m)
            z = pb[zb]
            P.op("pe", C("matmul", z[:, lo:512], lhsT=KT[:, m * 128:(m + 1) * 128], rhs=q[:, lo:512], start=True, stop=False), reads=[kKT, kQH], writes=[f"pb{zb}"])
            P.op("pe", C("matmul", z[:, lo:512], lhsT=esel[:, m // 2, :], rhs=selT[:, lo:512], start=False, stop=(m < 8 * k0)), reads=["esel", "selT"], writes=[f"pb{zb}"])
            if m >= 8 * k1:
                P.op("pe", C("matmul", z[:, 256:512], lhsT=ident, rhs=masks[:, 8 + m - 8 * k1, :], start=False, stop=True), reads=["cm", "masks"], writes=[f"pb{zb}"])
            elif m >= 8 * k0:
                P.op("pe", C("matmul", z[:, 0:256], lhsT=ident, rhs=masks[:, 8 + m - 8 * k0, :], start=False, stop=True), reads=["cm", "masks"], writes=[f"pb{zb}"])
            P.op("act", C("activation", out=Wb[:, s_, lo:512], in_=z[:, lo:512], func=AF.Exp, bias=rb31c[:, h:h + 1]), reads=[f"pb{zb}", "rb31c"], writes=[f"Wb{s_}"])
            for half, kk in ((0, k0), (1, k1)):
                mm15 = m - 8 * kk + 7
                if 0 <= mm15 <= 14:
                    c0 = half * 256
                    P.op("dve", C("tensor_tensor", out=Wb[:, s_, c0:c0 + 256], in0=Wb[:, s_, c0:c0 + 256], in1=EBs[:, 128 * mm15:128 * mm15 + 256], op=ALU.mult), reads=[f"Wb{s_}", "EBs"], writes=[f"Wb{s_}"])

        def mB(m):
            s_ = m % 6; lo = rng(m)
            P.op("pe", C("matmul", po[:, lo:512], lhsT=VH[:, m, :], rhs=Wb[:, s_, lo:512], start=(m == 0), stop=(m == ntile - 1)), reads=[kVH, f"Wb{s_}"], writes=["pb6"])
            P.op("pe", C("matmul", pl[:, lo:512], lhsT=ones, rhs=Wb[:, s_, lo:512], start=(m == 0), stop=(m == ntile - 1)), reads=["cm", f"Wb{s_}"], writes=["pb7"])

        for m in range(ntile + 2):
            if m < ntile:
                mA(m)
            if 0 <= m - 2 < ntile:
                mB(m - 2)
        P.op("dve", C("reciprocal", out=rl[:], in_=pl[:]), reads=["pb7"], writes=["rl"])
        P.op("dve", C("tensor_tensor", out=OT[:, k0 * 256:k0 * 256 + 512], in0=po[:], in1=rl[:], op=ALU.mult), reads=["pb6", "rl"], writes=["OT"])

    for h in range(8):
        hh = 8 + h
        if hh + 1 < 16:
            load_head(hh + 1)
        KT = KTa[:, hh % 2, :]; kKT = f"KT{hh % 2}"
        P.dma("sp", C("dma_start", out=Hk[:, 0:2048], in_=AP(tensor=bvs.tensor, offset=h * LBV, ap=[[1, 128], [1, 2048]])), reads=["bvs"], writes=["Hk"])
        P.op("act", C("activation", out=EBs[:], in_=Hk[:, 0:2048], func=AF.Exp, bias=nrb31c[:, h:h + 1]), reads=["Hk", "nrb31c"], writes=["EBs"])
        P.op("dve", C("tensor_reduce", out=kmf[:], in_=KT[:].rearrange("p (n s) -> p n s", s=256), axis=AX.X, op=ALU.add), reads=[kKT], writes=["kmf"])
        P.op("dve", C("tensor_copy", out=kmb[:], in_=kmf[:]), reads=["kmf"], writes=["kmb"])
        for kp in range(4):
            moba_pair(h, kp)
        store_head(hh)
    P.barrier()
    A2.close()

    A3 = Alloc(nc)
    BW = A3.sb("BW3", [128, 16, 2048], BF16)
    XG3 = A3.sb("XG3", [128, 16, 512], BF16)
    OG = A3.sb("OG", [128, 16, 512], BF16)
    MT = A3.sb("MT", [128, 16, 512], BF16)
    WG = A3.sb("WG", [128, 2, 2, 16, 128], BF16)
    WB = A3.sb("WB", [128, 2, 2, 8, 128], BF16)
    GS = A3.sb("GS", [128, 2, 512], F32)
    YS = A3.sb("YS", [128, 512], F32)
    XO = A3.sb("XO", [128, D], F32)
    H1 = A3.sb("H1", [128, D], F32)
    X1B = A3.sb("X1B", [128, D], BF16)
    XTS = A3.sb("XTS", [128, 16, 128], BF16)
    LNP = A3.sb("LNP", [128, 2, D], F32)
    st6 = A3.sb("st6", [128, 4, 6], F32)
    mv = A3.sb("mv", [128, 4], F32)

    def layernorm(src_key, src, dst_key, dst):
        for c in range(4):
            P.op("dve", C("bn_stats", out=st6[:, c, :], in_=src[:, c * 512:(c + 1) * 512]), reads=[src_key], writes=["st6"])
        P.op("dve", C("bn_aggr", out=mv[:, 0:2], in_=st6[:].rearrange("p a b -> p (a b)")), reads=["st6"], writes=["mv"])
        P.op("act", C("activation", out=mv[:, 2:3], in_=mv[:, 1:2], func=AF.Sqrt, bias=EPS), reads=["mv"], writes=["mv"])
        P.op("dve", C("reciprocal", out=mv[:, 3:4], in_=mv[:, 2:3]), reads=["mv"], writes=["mv"])
        P.op("dve", C("tensor_scalar", out=dst, in0=src, scalar1=mv[:, 0:1], scalar2=mv[:, 3:4], op0=ALU.subtract, op1=ALU.mult), reads=[src_key, "mv"], writes=[dst_key])
        P.op("dve", C("tensor_tensor", out=dst, in0=dst, in1=LNP[:, 0, :], op=ALU.mult), reads=[dst_key, "LNP"], writes=[dst_key])
        P.op("dve", C("tensor_tensor", out=dst, in0=dst, in1=LNP[:, 1, :], op=ALU.add), reads=[dst_key, "LNP"], writes=[dst_key])

    for dc in range(16):
        P.dma("pool", C("dma_start", out=BW[:, dc, :], in_=w_out[dc * 128:(dc + 1) * 128, :]), writes=["BW3"], chan="BW3")
    for j in range(2):
        P.dma("sp", C("dma_start", out=LNP[:, j, :], in_=AP(tensor=ln_d.tensor, offset=j * D, ap=[[0, 128], [1, D]])), writes=["LNP"], chan="LNP")
    for g in range(4):
        for dc in range(16):
            P.dma("pool", C("dma_start", out=XG3[:, dc, :], in_=xT_own[dc * 128:(dc + 1) * 128, g * 512:(g + 1) * 512]), writes=["XG3"], chan="XG3")
        P.dma("sp", C("dma_start", out=OG[:], in_=oTs[:, :, g * 512:(g + 1) * 512].rearrange("h p t -> p h t")), writes=["OG"])
        for fc in range(16):
            sl = fc % 2
            for j in range(2):
                P.dma("pool", C("dma_start", out=WG[:, sl, j, :, :], in_=w_gate[:, j * 2048 + fc * 128:j * 2048 + (fc + 1) * 128].rearrange("(c p) f -> p c f", p=128)),
                      writes=[f"WG{sl}"], chan=f"WG{sl}")
                wsrc = wbs if j == 0 else wbm
                P.dma("pool", C("dma_start", out=WB[:, sl, j, :, :], in_=wsrc[:, fc * 128:(fc + 1) * 128].rearrange("(c p) f -> p c f", p=128)),
                      writes=[f"WB{sl}"], chan=f"WB{sl}")
            for j in range(2):
                pgt = pb[j]; pyt = pb[2 + j]
                for dc in range(16):
                    P.op("pe", C("matmul", pgt[:], lhsT=WG[:, sl, j, dc, :], rhs=XG3[:, dc, :], start=(dc == 0), stop=(dc == 15)), reads=[f"WG{sl}", "XG3"], writes=[f"pb{j}"])
                P.op("act", C("activation", out=GS[:, j, :], in_=pgt[:], func=AF.Sigmoid, bias=bgc[:, j * 16 + fc:j * 16 + fc + 1]), reads=[f"pb{j}", "bgc"], writes=[f"GS{j}"])
                for hc in range(8):
                    P.op("pe", C("matmul", pyt[:], lhsT=WB[:, sl, j, hc, :], rhs=OG[:, j * 8 + hc, :], start=(hc == 0), stop=(hc == 7)), reads=[f"WB{sl}", "OG"], writes=[f"pb{2 + j}"])
            P.op("dve", C("tensor_tensor", out=YS[:], in0=pb[2][:], in1=GS[:, 0, :], op=ALU.mult), reads=["pb2", "GS0"], writes=["YS"])
            P.op("dve", C("tensor_tensor", out=GS[:, 1, :], in0=pb[3][:], in1=GS[:, 1, :], op=ALU.mult), reads=["pb3", "GS1"], writes=["GS1"])
            P.op("dve", C("tensor_tensor", out=MT[:, fc, :], in0=YS[:], in1=GS[:, 1, :], op=ALU.add), reads=["YS", "GS1"], writes=["MT"])
        for tt in range(4):
            ti = g * 4 + tt
            P.dma("sp", C("dma_start", out=XO[:], in_=x_own[ti * 128:(ti + 1) * 128, :]), writes=["XO"])
            for fg in range(4):
                ps = pb[4 + fg]
                for fc in range(16):
                    P.op("pe", C("matmul", ps[:], lhsT=MT[:, fc, tt * 128:(tt + 1) * 128], rhs=BW[:, fc, fg * 512:(fg + 1) * 512], start=(fc == 0), stop=(fc == 15)),
                         reads=["MT", "BW3"], writes=[f"pb{4 + fg}"])
                P.op("dve", C("scalar_tensor_tensor", out=H1[:, fg * 512:(fg + 1) * 512], in0=XO[:, fg * 512:(fg + 1) * 512], scalar=ALPHA, in1=ps[:], op0=ALU.mult, op1=ALU.add),
                     reads=["XO", f"pb{4 + fg}"], writes=["H1"])
            layernorm("H1", H1[:], "H1", H1[:])
            P.dma("sp", C("dma_start", out=x1S[ti], in_=H1[:]), reads=["H1"], writes=["x1S"], chan="H1_st")
            if debug:
                P.dma("sp", C("dma_start", out=dbg_x1[ti * 128:(ti + 1) * 128, :], in_=H1[:]), reads=["H1"], writes=["dbg_x1"], chan="H1_st2")
            P.op("act", C("activation", out=X1B[:], in_=H1[:], func=AF.Copy), reads=["H1"], writes=["X1B"])
            for dcg in range(4):
                ptv = pb[dcg % 2][:, 0:256].bitcast(BF16)
                for j in range(4):
                    dc = dcg * 4 + j
                    P.op("pe", C("transpose", ptv[:, j * 128:(j + 1) * 128], X1B[:, dc * 128:(dc + 1) * 128], ident), reads=["X1B", "cm"], writes=[f"pb{dcg % 2}"])
                evac(XTS[:, dcg * 4:(dcg + 1) * 4, :].rearrange("p a b -> p (a b)"), ptv, [f"pb{dcg % 2}"], ["XTS"])
            P.dma("sp", C("dma_start", out=x1Ts[ti], in_=XTS[:]), reads=["XTS"], writes=["x1Ts"], chan="XTS_st")
    P.barrier()
    A3.close()

    A4 = Alloc(nc)
    X1T = A4.sb("X1T", [128, 2, 16, 128], BF16)
    Wk = A4.sb("Wk", [128, 16, 2048], BF16)
    Sc = A4.sb("Sc", [128, 2, 16, 128], F32)
    tops = A4.sb("tops", [128, 16, 16], F32)
    idxu = A4.sb("idxu", [128, 16, 16], U32)
    idxf = A4.sb("idxf", [128, 16, 16], F32)
    cand = A4.sb("cand", [128, 256], F32)
    cjk = A4.sb("cjk", [128, 256], F32)
    best = A4.sb("best", [128, 8, 16], F32)
    best2 = A4.sb("best2", [128, 8, 16], F32)
    posu = A4.sb("posu", [128, 128], U32)
    posf = A4.sb("posf", [128, 128], F32)
    pti = A4.sb("pti", [128, 128], mybir.dt.int32)
    paf = A4.sb("paf", [128, 128], F32)
    pr = A4.sb("pr", [128, 128], F32)
    pneg = A4.sb("pneg", [128, 128], F32)
    pbf = A4.sb("pbf", [128, 128], F32)
    e4 = A4.sb("e4", [128, 128, 16], BF16)
    nb0 = A4.sb("nb0", [128, 8], F32)
    gsum = A4.sb("gsum", [128, 8], F32)
    gw = A4.sb("gw", [128, 8, 16], F32)
    exi = A4.sb("exi", [128, 128], F32)
    exj = A4.sb("exj", [128, 128], F32)
    ijgT = A4.sb("ijgT", [128, 3, 128], F32)
    iotaF = misc[:, 48:176]
    iota16 = misc[:, 0:16]

    def acopy(out, in_, reads, writes):
        P.op("act", C("activation", out=out, in_=in_, func=AF.Copy), reads=reads, writes=writes)

    A4p = Alloc(nc)
    WT = A4p.sb("WT", [128, 1, D], F32)
    KYF = A4p.sb("KYF", [128, 256], F32)
    P.dma("sp", C("dma_start", out=KYF[:], in_=keysT_d), writes=["KYF"])
    for c in range(16):
        sl = 0
        P.dma("sp", C("dma_start", out=WT[:, sl, :], in_=wpqT[c * 128:(c + 1) * 128, :]), writes=[f"WT{sl}"])
        for d4 in range(4):
            bk = (c * 4 + d4) % 4
            for q_ in range(4):
                dc = d4 * 4 + q_
                P.op("pe", C("matmul", pb[bk][:, q_ * 128:(q_ + 1) * 128], lhsT=WT[:, sl, dc * 128:(dc + 1) * 128], rhs=KYF[:, (c % 2) * 128:(c % 2 + 1) * 128], start=True, stop=True),
                     reads=[f"WT{sl}", "KYF"], writes=[f"pb{bk}"])
            acopy(Wk[:, d4 * 4:(d4 + 1) * 4, c * 128:(c + 1) * 128], pb[bk][:].rearrange("p (a b) -> p a b", a=4), [f"pb{bk}"], ["Wk"])

    P.barrier()
    A4p.close()
    A4b = Alloc(nc)
    OHI = A4b.sb("OHI", [128, 2, 64, 128], BF16)
    OHJ = A4b.sb("OHJ", [128, 2, 64, 128], BF16)
    W3 = A4b.sb("W3", [128, 2, 128, 64], BF16)
    cand3 = cand[:].rearrange("p (a b) -> p a b", a=16)

    def p4a_S(ti):
        sl = ti % 2
        P.dma("sp", C("dma_start", out=X1T[:, sl, :, :], in_=x1Ts[ti]), writes=[f"X1T{sl}"])
        for fg in range(4):
            bk = 4 + (fg % 2)
            for dc in range(16):
                P.op("pe", C("matmul", pb[bk][:], lhsT=X1T[:, sl, dc, :], rhs=Wk[:, dc, fg * 512:(fg + 1) * 512], start=(dc == 0), stop=(dc == 15)), reads=[f"X1T{sl}", "Wk"], writes=[f"pb{bk}"])
            acopy(Sc[:, sl, fg * 4:(fg + 1) * 4, :].rearrange("p a b -> p (a b)"), pb[bk][:], [f"pb{bk}"], [f"Sc{sl}"])

    def p4a_X1(ti):
        sl = ti % 2
        sk = f"Sc{sl}"
        for c in range(16):
            P.op("dve", C("max", out=tops[:, c, 0:8], in_=Sc[:, sl, c, :]), reads=[sk], writes=["tops"])
            P.op("dve", C("max_index", out=idxu[:, c, 0:8], in_max=tops[:, c, 0:8], in_values=Sc[:, sl, c, :]), reads=[sk, "tops"], writes=["idxu"])
            P.op("dve", C("match_replace", out=Sc[:, sl, c, :], in_to_replace=tops[:, c, 0:8], in_values=Sc[:, sl, c, :], imm_value=-1e30), reads=[sk, "tops"], writes=[sk])
            P.op("dve", C("max", out=tops[:, c, 8:16], in_=Sc[:, sl, c, :]), reads=[sk], writes=["tops"])
            P.op("dve", C("max_index", out=idxu[:, c, 8:16], in_max=tops[:, c, 8:16], in_values=Sc[:, sl, c, :]), reads=[sk, "tops"], writes=["idxu"])
        P.op("dve", C("tensor_copy", out=idxf[:], in_=idxu[:]), reads=["idxu"], writes=["idxf"])
        for h in range(8):
            c0 = 2 * h; c1 = 2 * h + 1
            a_bc = tops[:, c0, :].rearrange("p (a o) -> p a o", o=1).broadcast_to([128, 16, 16])
            b_bc = tops[:, c1:c1 + 1, :].to_broadcast([128, 16, 16])
            P.op("dve", C("tensor_tensor", out=cand3, in0=a_bc, in1=b_bc, op=ALU.add), reads=["tops"], writes=["cand"])
            P.op("dve", C("max", out=best[:, h, 0:8], in_=cand[:]), reads=["cand"], writes=["best"])
            P.op("dve", C("match_replace", out=cjk[:], in_to_replace=best[:, h, 0:8], in_values=cand[:], imm_value=-1e30), reads=["cand", "best"], writes=["cjk"])
            P.op("dve", C("max", out=best[:, h, 8:16], in_=cjk[:]), reads=["cjk"], writes=["best"])
            P.op("dve", C("max_index", out=posu[:, h * 16:h * 16 + 8], in_max=best[:, h, 0:8], in_values=cand[:]), reads=["cand", "best"], writes=["posu"])
            P.op("dve", C("max_index", out=posu[:, h * 16 + 8:h * 16 + 16], in_max=best[:, h, 8:16], in_values=cand[:]), reads=["cand", "best"], writes=["posu"])
        P.op("dve", C("tensor_copy", out=posf[:], in_=posu[:]), reads=["posu"], writes=["posf"])
        P.op("dve", C("tensor_scalar", out=pti[:], in0=posf[:], scalar1=1.0 / 16.0, scalar2=None, op0=ALU.mult), reads=["posf"], writes=["pti"])
        P.op("dve", C("tensor_copy", out=paf[:], in_=pti[:]), reads=["pti"], writes=["paf"])
        P.op("dve", C("scalar_tensor_tensor", out=pr[:], in0=paf[:], scalar=-16.0, in1=posf[:], op0=ALU.mult, op1=ALU.add), reads=["paf", "posf"], writes=["pr"])
        P.op("dve", C("tensor_scalar", out=pneg[:], in0=pr[:], scalar1=0.0, scalar2=None, op0=ALU.is_lt), reads=["pr"], writes=["pneg"])
        P.op("dve", C("tensor_tensor", out=paf[:], in0=paf[:], in1=pneg[:], op=ALU.subtract), reads=["paf", "pneg"], writes=["paf"])
        P.op("dve", C("scalar_tensor_tensor", out=pbf[:], in0=pneg[:], scalar=16.0, in1=pr[:], op0=ALU.mult, op1=ALU.add), reads=["pneg", "pr"], writes=["pbf"])
        io16 = iota16.rearrange("p (o a) -> p o a", o=1).to_broadcast([128, 128, 16])
        for side, (src, dst, key) in enumerate([(paf, exi, "exi"), (pbf, exj, "exj")]):
            P.op("dve", C("tensor_tensor", out=e4[:], in0=io16, in1=src[:].rearrange("p (s o) -> p s o", o=1).broadcast_to([128, 128, 16]), op=ALU.is_equal), reads=["misc", "paf", "pbf"], writes=["e4"])
            for h in range(8):
                P.op("dve", C("tensor_tensor", out=e4[:, h * 16:(h + 1) * 16, :], in0=e4[:, h * 16:(h + 1) * 16, :], in1=idxf[:, 2 * h + side:2 * h + side + 1, :].to_broadcast([128, 16, 16]), op=ALU.mult),
                     reads=["e4", "idxf"], writes=["e4"])
            P.op("dve", C("tensor_reduce", out=dst[:], in_=e4[:], axis=AX.X, op=ALU.add), reads=["e4"], writes=[key])
        P.op("dve", C("tensor_scalar", out=nb0[:], in0=best[:, :, 0], scalar1=-1.0, scalar2=None, op0=ALU.mult), reads=["best"], writes=["nb0"])
        P.op("dve", C("tensor_copy", out=best2[:], in_=best[:]), reads=["best"], writes=["best2"])

    def p4a_X1b(ti):
        for h in range(8):
            P.op("act", C("activation", out=gw[:, h, :], in_=best2[:, h, :], func=AF.Exp, bias=nb0[:, h:h + 1], accum_out=gsum[:, h:h + 1]), reads=["best2", "nb0"], writes=["gw", "gsum"])
        P.op("dve", C("reciprocal", out=gsum[:], in_=gsum[:]), reads=["gsum"], writes=["gsum"])
        P.op("dve", C("tensor_tensor", out=gw[:], in0=gw[:], in1=gsum[:].rearrange("p (h o) -> p h o", o=1).broadcast_to([128, 8, 16]), op=ALU.mult), reads=["gw", "gsum"], writes=["gw"])

    def p4a_T(ti):
        pT = pb[6]
        for q_, (src, key) in enumerate([(exi[:], "exi"), (exj[:], "exj"), (gw[:].rearrange("p h k -> p (h k)"), "gw")]):
            P.op("pe", C("transpose", pT[:, q_ * 128:(q_ + 1) * 128], src, identf[:]), reads=[key, "identf"], writes=["pb6"])
        acopy(ijgT[:].rearrange("p a b -> p (a b)"), pT[:, 0:384], ["pb6"], ["ijgT"])

    def p4a_OH(ti, hf):
        t0 = hf * 64
        iota3 = iotaF.rearrange("p (o i) -> p o i", o=1).to_broadcast([128, 64, 128])
        bc = lambda row: ijgT[:, row, t0:t0 + 64].rearrange("p (t o) -> p t o", o=1).broadcast_to([128, 64, 128])
        P.op("dve", C("tensor_tensor", out=OHI[:, hf, :, :], in0=iota3, in1=bc(0), op=ALU.is_equal), reads=["misc", "ijgT"], writes=[f"OHI{hf}"])
        P.op("dve", C("tensor_tensor", out=OHJ[:, hf, :, :], in0=iota3, in1=bc(1), op=ALU.is_equal), reads=["misc", "ijgT"], writes=[f"OHJ{hf}"])
        P.op("dve", C("tensor_tensor", out=OHI[:, hf, :, :], in0=OHI[:, hf, :, :], in1=bc(2), op=ALU.mult), reads=[f"OHI{hf}", "ijgT"], writes=[f"OHI{hf}"])

    def p4a_Y(ti, hf):
        for t4 in range(16):
            bk = t4 % 4
            for tq in range(4):
                t = t4 * 4 + tq
                P.op("pe", C("matmul", pb[bk][:, tq * 128:(tq + 1) * 128], lhsT=OHI[:, hf, t, :], rhs=OHJ[:, hf, t, :], start=True, stop=True), reads=[f"OHI{hf}", f"OHJ{hf}"], writes=[f"pb{bk}"])
            acopy(W3[:, hf, :, t4 * 4:(t4 + 1) * 4], pb[bk][:].rearrange("p (t j) -> p j t", t=4), [f"pb{bk}"], [f"W3{hf}"])
        P.dma("sp", C("dma_start", out=Wd[ti, hf].rearrange("i j t -> i (j t)"), in_=W3[:, hf, :, :].rearrange("p j t -> p (j t)")), reads=[f"W3{hf}"], writes=["Wd"], chan=f"W3{hf}_st")

    p4a_S(0); p4a_X1(0); p4a_X1b(0); p4a_T(0); p4a_S(1)
    for ti in range(16):
        p4a_OH(ti, 0); p4a_OH(ti, 1)
        if ti + 1 < 16:
            p4a_X1(ti + 1)
        p4a_Y(ti, 0); p4a_Y(ti, 1)
        if ti + 1 < 16:
            p4a_X1b(ti + 1)
            p4a_T(ti + 1)
        if ti + 2 < 16:
            p4a_S(ti + 2)
    P.barrier()
    A4b.close()
    A4.close()

    A5 = Alloc(nc)
    X1TB = A5.sb("X1TB", [128, 16, 1024], BF16)
    ACCB = A5.sb("ACCB", [128, 8, D], F32)
    UB = A5.sb("UB", [128, 2, 16, 512], BF16)
    VB = A5.sb("VB", [128, 2, 4, D], BF16)
    WBk = A5.sb("WBk", [128, 8, 2, 4, 64], BF16)
    GB = A5.sb("GB", [128, 4, 1024], BF16)
    GT = A5.sb("GT", [128, 2, 512], BF16)
    VS = A5.sb("VS", [128, 1, D], F32)
    LNP = A5.sb("LNP4", [128, 2, D], F32)
    st6 = A5.sb("st64", [128, 4, 6], F32)
    mv = A5.sb("mv4", [128, 4], F32)

    def layernorm4(src_key, src, dst_key, dst):
        for c in range(4):
            P.op("dve", C("bn_stats", out=st6[:, c, :], in_=src[:, c * 512:(c + 1) * 512]), reads=[src_key], writes=["st64"])
        P.op("dve", C("bn_aggr", out=mv[:, 0:2], in_=st6[:].rearrange("p a b -> p (a b)")), reads=["st64"], writes=["mv4"])
        P.op("act", C("activation", out=mv[:, 2:3], in_=mv[:, 1:2], func=AF.Sqrt, bias=EPS), reads=["mv4"], writes=["mv4"])
        P.op("dve", C("reciprocal", out=mv[:, 3:4], in_=mv[:, 2:3]), reads=["mv4"], writes=["mv4"])
        P.op("dve", C("tensor_scalar", out=dst, in0=src, scalar1=mv[:, 0:1], scalar2=mv[:, 3:4], op0=ALU.subtract, op1=ALU.mult), reads=[src_key, "mv4"], writes=[dst_key])
        P.op("dve", C("tensor_tensor", out=dst, in0=dst, in1=LNP[:, 0, :], op=ALU.mult), reads=[dst_key, "LNP4"], writes=[dst_key])
        P.op("dve", C("tensor_tensor", out=dst, in0=dst, in1=LNP[:, 1, :], op=ALU.add), reads=[dst_key, "LNP4"], writes=[dst_key])

    for j in range(2):
        P.dma("sp", C("dma_start", out=LNP[:, j, :], in_=AP(tensor=ln_d.tensor, offset=(2 + j) * D, ap=[[0, 128], [1, D]])), writes=["LNP4"], chan="LNP4")
    hcnt = [0]
    vcnt = [0]
    for tb in range(2):
        for tt in range(8):
            ti = tb * 8 + tt
            P.dma("sp", C("dma_start", out=X1TB[:, :, tt * 128:(tt + 1) * 128], in_=x1Ts[ti]), writes=["X1TB"], chan="X1TB")
            P.dma("act", C("dma_start", out=ACCB[:, tt, :], in_=x1S[ti]), writes=[f"ACC{tt}"])
            P.op("act", C("activation", out=ACCB[:, tt, :], in_=ACCB[:, tt, :], func=AF.Copy, scale=ALPHA), reads=[f"ACC{tt}"], writes=[f"ACC{tt}"])
        for eb in range(32):
            sl = eb % 2
            j0 = eb * 4
            for dc in range(16):
                P.dma("pool", C("dma_start", out=UB[:, sl, dc, :], in_=upT[dc * 128:(dc + 1) * 128, j0 * 128:(j0 + 4) * 128]), writes=[f"UB{sl}"], chan=f"UB{sl}")
            for jj in range(4):
                vs_ = 0
                P.dma("sp", C("dma_start", out=VS[:, vs_, :], in_=vp[(j0 + jj) * 128:(j0 + jj + 1) * 128, :]), writes=[f"VS{vs_}"])
                P.op("act", C("activation", out=VB[:, sl, jj, :], in_=VS[:, vs_, :], func=AF.Copy), reads=[f"VS{vs_}"], writes=[f"VB{sl}"])
            for hf in range(2):
                P.dma("sp", C("dma_start", out=WBk[:, :, hf, :, :], in_=Wd[tb * 8:(tb + 1) * 8, hf, :, j0:j0 + 4, :].rearrange("n i j t -> i n j t")), writes=["WBk"])
            for jj in range(4):
                for half in range(2):
                    hb = hcnt[0] % 4; hcnt[0] += 1
                    ph = pb[hb]
                    for dc in range(16):
                        P.op("pe", C("matmul", ph[:], lhsT=UB[:, sl, dc, jj * 128:(jj + 1) * 128], rhs=X1TB[:, dc, half * 512:(half + 1) * 512], start=(dc == 0), stop=(dc == 15)),
                             reads=[f"UB{sl}", "X1TB"], writes=[f"pb{hb}"])
                    gs = hb % 2
                    P.op("act", C("activation", out=GT[:, gs, :], in_=ph[:], func=AF.Gelu), reads=[f"pb{hb}"], writes=[f"GT{gs}"])
                    P.op("dve", C("tensor_tensor", out=GB[:, jj, half * 512:(half + 1) * 512].rearrange("p (n h t) -> p n h t", n=4, h=2), in0=GT[:, gs, :].rearrange("p (n h t) -> p n h t", n=4, h=2),
                                  in1=WBk[:, half * 4:(half + 1) * 4, :, jj, :], op=ALU.mult), reads=[f"GT{gs}", "WBk"], writes=[f"GB{jj}"])
            for tt in range(8):
                for fg in range(4):
                    po = pb[4 + fg]
                    for jj in range(4):
                        P.op("pe", C("matmul", po[:], lhsT=GB[:, jj, tt * 128:(tt + 1) * 128], rhs=VB[:, sl, jj, fg * 512:(fg + 1) * 512], start=(jj == 0), stop=(jj == 3)),
                             reads=[f"GB{jj}", f"VB{sl}"], writes=[f"pb{4 + fg}"])
                    P.op("dve", C("tensor_tensor", out=ACCB[:, tt, fg * 512:(fg + 1) * 512], in0=ACCB[:, tt, fg * 512:(fg + 1) * 512], in1=po[:], op=ALU.add), reads=[f"ACC{tt}", f"pb{4 + fg}"], writes=[f"ACC{tt}"])
        for tt in range(8):
            ti = tb * 8 + tt
            layernorm4(f"ACC{tt}", ACCB[:, tt, :], f"ACC{tt}", ACCB[:, tt, :])
            P.dma("sp", C("dma_start", out=out_d[ti * 128:(ti + 1) * 128, :], in_=ACCB[:, tt, :]), reads=[f"ACC{tt}"], writes=["out"], chan=f"ACC{tt}_st")
    P.barrier()
    A5.close()
    P.emit()
    return nc


def _bucket(rel):
    rel = np.maximum(rel, 0)
    logd = np.log(np.maximum(rel, 1).astype(np.float32) / np.float32(16)) / np.float32(np.log(1024 / 16))
    large = 16 + (logd.astype(np.float32) * np.float32(16)).astype(np.int32)
    large = np.minimum(large, 31)
    return np.where(rel < 16, rel, large)


_NC = {}


def kernel(x, w_in, w_gate, b_gate, w_branch_sb, w_branch_moba, w_out, rel_bias, ln1_g, ln1_b,
           w_peer_query, peer_sub_keys, peer_u, peer_v, ln2_g, ln2_b, _debug=False):
    f = np.float32
    x = np.asarray(x, f)
    w_in = np.asarray(w_in, f)
    c_ = np.ascontiguousarray
    wq = c_(np.concatenate([w_in[:, 0:1024], w_in[:, 3072:4096]], axis=1))
    wk = c_(np.concatenate([w_in[:, 1024:2048], w_in[:, 4096:5120]], axis=1))
    wv = c_(np.concatenate([w_in[:, 2048:3072], w_in[:, 5120:6144]], axis=1))
    bgc = c_(np.asarray(b_gate, f).reshape(32, 128).T)
    rbT = c_(np.asarray(rel_bias, f).T)
    rb31 = c_(np.asarray(rel_bias, f)[:, 31].reshape(1, 8))
    lnp = c_(np.stack([ln1_g, ln1_b, ln2_g, ln2_b]).astype(f))
    psk = np.asarray(peer_sub_keys, f)
    keysT = c_(np.concatenate([psk[0].T, psk[1].T], axis=1))
    cmat = np.zeros((128, 512), f)
    cmat[:, 0:128] = np.eye(128)
    cmat[:, 128:256] = 1.0
    cmat[:, 256:384] = -1.0
    jj, ss = np.meshgrid(np.arange(128), np.arange(128), indexing="ij")
    cmat[:, 384:512] = -(jj >= ss).astype(f)
    esel = np.zeros((32, 32, 128), f)
    for n in range(32):
        esel[n, n, :] = 1.0
    esel = esel.reshape(32, 32 * 128)
    misc = np.zeros((128, 176), f)
    misc[:, 48:176] = np.arange(128)[None, :]
    misc[:, 0:32] = np.arange(32)[None, :]
    misc[:, 32:40] = np.arange(128)[:, None] + 128 * np.arange(8)[None, :]
    xTs = [c_(x[b].T) for b in range(2)]
    wgate = c_(np.asarray(w_gate, f)); wbs = c_(np.asarray(w_branch_sb, f)); wbm = c_(np.asarray(w_branch_moba, f))
    wo = c_(np.asarray(w_out, f)); wpqT = c_(np.asarray(w_peer_query, f).T)
    pu = c_(np.asarray(peer_u, f).reshape(128, 128, D).transpose(2, 1, 0).reshape(D, 16384))
    pv = c_(np.asarray(peer_v, f).reshape(128, 128, D).transpose(1, 0, 2).reshape(16384, D))
    in_maps = []
    toks = []
    for c in range(8):
        b, r = c // 4, c % 4
        tok = np.concatenate([256 * (4 * k + r) + np.arange(255, -1, -1) for k in range(8)])
        toks.append((b, tok))
        x_own = c_(x[b][tok])
        w = np.arange(LBV)
        ohx = (_bucket(256 * r + 1151 - w)[None, :] == np.arange(32)[:, None]).astype(f)
        qblk = (tok // 256).astype(f)
        in_maps.append(dict(
            xT_b=xTs[b], x_own=x_own, xT_own=c_(x_own.T), wq=wq, wk=wk, wv=wv, w_gate=wgate, bgc=bgc,
            wbs=wbs, wbm=wbm, w_out=wo, rbT=rbT, rb31=rb31, ohx=c_(ohx), lnp=lnp, wpqT=wpqT, keysT=keysT,
            upT=pu, vp=pv, qpos=c_(tok.astype(f).reshape(1, NT)), qblkc=c_(qblk.reshape(16, 128).T),
            cmat=cmat, esel=esel, misc=misc))
    key = bool(_debug)
    if key not in _NC:
        _NC[key] = build(debug=key)
    res = run_bass_kernel_spmd(_NC[key], in_maps, core_ids=list(range(8)))
    out = np.zeros((2, S, D), f)
    for c in range(8):
        b, tok = toks[c]
        out[b][tok] = res.results[c]["out"]
    if _debug:
        return out, res, toks
    return out
```

```python
import numpy as np
import concourse.bass as bass
import concourse.mybir as mybir
from concourse.bass_utils import run_bass_kernel_spmd
from concourse.bass_types import AP

F32 = mybir.dt.float32
BF16 = mybir.dt.bfloat16
U32 = mybir.dt.uint32
AF = mybir.ActivationFunctionType
ALU = mybir.AluOpType
AX = mybir.AxisListType

ENGS = ["pe", "act", "dve", "pool", "sp"]
D = 2048
S = 8192
NT = 2048
ALPHA = 2.0 ** 0.25
EPS = 1e-5
NEGB = -30000.0
LBV = 2176


class Prog:
    def __init__(self, nc):
        self.nc = nc
        self.ops = {e: [] for e in ENGS}
        self.cnt = {e: 0 for e in ENGS}
        self.sems = {}
        self.dma_cnt = {}
        self.last_w = {}
        self.readers = {}
        self.waited = {e: {} for e in ENGS}
        self._stack = []

    def sem(self, key):
        if key not in self.sems:
            cm = self.nc.semaphore("s_" + key)
            self.sems[key] = cm.__enter__()
            self._stack.append(cm)
        return self.sems[key]

    def _deps(self, eng, reads, writes):
        deps = {}

        def add(d):
            if d is not None and deps.get(d[0], 0) < d[1]:
                deps[d[0]] = d[1]
        for b in reads:
            add(self.last_w.get(b))
        for b in writes:
            add(self.last_w.get(b))
            for r in self.readers.get(b, []):
                add(r)
        waits = []
        for k, v in deps.items():
            if k == "e_pe" and eng == "pe":
                continue
            if self.waited[eng].get(k, 0) >= v:
                continue
            self.waited[eng][k] = v
            self.sem(k)
            waits.append((k, v))
        return waits

    def _commit(self, reads, writes, me):
        for b in reads:
            self.readers.setdefault(b, []).append(me)
        for b in writes:
            self.last_w[b] = me
            self.readers[b] = []

    def op(self, eng, fn, reads=(), writes=()):
        waits = self._deps(eng, reads, writes)
        self.cnt[eng] += 1
        me = ("e_" + eng, self.cnt[eng])
        self.sem(me[0])
        self.ops[eng].append((fn, waits, (me[0], 1)))
        self._commit(reads, writes, me)

    def dma(self, eng, fn, reads=(), writes=(), chan=None):
        if chan is None:
            chan = (list(writes) + list(reads))[0]
        waits = self._deps(eng, reads, writes)
        key = "d_" + chan
        self.dma_cnt[key] = self.dma_cnt.get(key, 0) + 16
        me = (key, self.dma_cnt[key])
        self.sem(key)
        self.ops[eng].append((fn, waits, (key, 16)))
        self._commit(reads, writes, me)

    def barrier(self, engs=ENGS):
        waits = [("e_" + e, self.cnt[e]) for e in ENGS if self.cnt[e] > 0]
        waits += list(self.dma_cnt.items())
        for e in engs:
            w = [(k, v) for k, v in waits if self.waited[e].get(k, 0) < v and k != "e_" + e]
            for k, v in w:
                self.waited[e][k] = v
            self.ops[e].append((None, w, None))
        if len(engs) == len(ENGS):
            self.last_w = {}
            self.readers = {}

    def emit(self):
        nc = self.nc
        P = self
        with nc.Block() as block:
            def run(e, name):
                for fn, waits, inc in P.ops[name]:
                    for k, v in waits:
                        e.wait_ge(P.sems[k], v)
                    if fn is not None:
                        fn(e).then_inc(P.sems[inc[0]], inc[1])

            @block.sync
            def _(e):
                run(e, "sp")

            @block.scalar
            def _(e):
                run(e, "act")

            @block.vector
            def _(e):
                run(e, "dve")

            @block.gpsimd
            def _(e):
                run(e, "pool")

            @block.tensor
            def _(e):
                run(e, "pe")


def C(name, *a, **kw):
    return lambda e: getattr(e, name)(*a, **kw)


class Alloc:
    def __init__(self, nc):
        self.nc = nc
        self._stack = []

    def sb(self, name, shape, dt):
        cm = self.nc.sbuf_tensor("sb_" + name, shape, dt)
        t = cm.__enter__()
        self._stack.append(cm)
        return t

    def ps(self, name, shape, dt=F32):
        cm = self.nc.psum_tensor("ps_" + name, shape, dt)
        t = cm.__enter__()
        self._stack.append(cm)
        return t

    def close(self):
        for cm in reversed(self._stack):
            cm.__exit__(None, None, None)
        self._stack = []


def build(debug=False):
    nc = bass.Bass("TRN2", target_bir_lowering=False)

    def din(name, shape, dt=F32):
        return nc.dram_tensor(name, list(shape), dt, kind="ExternalInput").ap()

    def dscr(name, shape, dt):
        return nc.dram_tensor(name, list(shape), dt, kind="Internal").ap()

    xT_b = din("xT_b", [D, S])
    x_own = din("x_own", [NT, D])
    xT_own = din("xT_own", [D, NT])
    wq = din("wq", [D, 2048]); wk = din("wk", [D, 2048]); wv = din("wv", [D, 2048])
    w_gate = din("w_gate", [D, 4096]); bgc_d = din("bgc", [128, 32])
    wbs = din("wbs", [1024, D]); wbm = din("wbm", [1024, D]); w_out = din("w_out", [D, D])
    rbT_d = din("rbT", [32, 8]); rb31_d = din("rb31", [1, 8]); ohx_d = din("ohx", [32, LBV])
    ln_d = din("lnp", [4, D])
    wpqT = din("wpqT", [2048, D]); keysT_d = din("keysT", [128, 256])
    upT = din("upT", [D, 16384]); vp = din("vp", [16384, D])
    qpos_d = din("qpos", [1, NT]); qblkc_d = din("qblkc", [128, 16])
    cm_d = din("cmat", [128, 512]); esel_d = din("esel", [32, 32 * 128])
    misc_d = din("misc", [128, 176])
    out_d = nc.dram_tensor("out", [NT, D], F32, kind="ExternalOutput").ap()
    if debug:
        dbg_o = nc.dram_tensor("dbg_o", [16, 128, NT], BF16, kind="ExternalOutput").ap()
        dbg_x1 = nc.dram_tensor("dbg_x1", [NT, D], F32, kind="ExternalOutput").ap()

    kTs = dscr("kTs", [16, 128, S], BF16)
    vS = dscr("vS", [16, 128, 64, 128], BF16)
    qTs = dscr("qTs", [16, 128, NT], BF16)
    oTs = dscr("oTs", [16, 128, NT], BF16)
    bvs = dscr("bvs", [8, LBV], F32)
    x1S = dscr("x1S", [16, 128, D], F32)
    x1Ts = dscr("x1Ts", [16, 128, 16, 128], BF16)
    Wd = dscr("Wd", [16, 2, 128, 128, 64], BF16)

    P = Prog(nc)
    G = Alloc(nc)
    cm = G.sb("cm", [128, 4, 128], BF16)
    identf = G.sb("identf", [128, 128], F32)
    misc = G.sb("misc", [128, 176], F32)
    qposB = G.sb("qposB", [128, 256], F32)
    qblkc = G.sb("qblkc", [128, 16], F32)
    rb31c = G.sb("rb31c", [128, 8], F32)
    nrb31c = G.sb("nrb31c", [128, 8], F32)
    bgc = G.sb("bgc", [128, 32], F32)
    pb = [G.ps(f"pb{i}", [128, 512], F32) for i in range(8)]
    ident = cm[:, 0, :]; ones = cm[:, 1, :]; negones = cm[:, 2, :]; negtri = cm[:, 3, :]
    iota_n = misc[:, 0:32]

    P.dma("pool", C("dma_start", out=cm[:].rearrange("p a b -> p (a b)"), in_=cm_d), writes=["cm"])
    P.dma("sp", C("dma_start", out=identf[:], in_=cm_d[:, 0:128]), writes=["identf"])
    P.dma("sp", C("dma_start", out=misc[:], in_=misc_d), writes=["misc"])
    P.dma("sp", C("dma_start", out=qposB[:], in_=AP(tensor=qpos_d.tensor, offset=0, ap=[[0, 128], [1, 256]])), writes=["qposB"])
    P.dma("sp", C("dma_start", out=qblkc[:], in_=qblkc_d), writes=["qblkc"])
    P.dma("sp", C("dma_start", out=rb31c[:], in_=AP(tensor=rb31_d.tensor, offset=0, ap=[[0, 128], [1, 8]])), writes=["rb31c"])
    P.dma("sp", C("dma_start", out=bgc[:], in_=bgc_d), writes=["bgc"])
    P.op("dve", C("tensor_scalar", out=nrb31c[:], in0=rb31c[:], scalar1=-1.0, scalar2=None, op0=ALU.mult), reads=["rb31c"], writes=["nrb31c"])
    evc = [0]

    def evac(out, in_, reads, writes, scale=None):
        evc[0] += 1
        if evc[0] % 2 == 0:
            if scale is None:
                P.op("act", C("activation", out=out, in_=in_, func=AF.Copy), reads=reads, writes=writes)
            else:
                P.op("act", C("activation", out=out, in_=in_, func=AF.Copy, scale=scale), reads=reads, writes=writes)
        else:
            if scale is None:
                P.op("dve", C("tensor_copy", out=out, in_=in_), reads=reads, writes=writes)
            else:
                P.op("dve", C("tensor_scalar", out=out, in0=in_, scalar1=scale, scalar2=None, op0=ALU.mult), reads=reads, writes=writes)

    A1 = Alloc(nc)
    BWa = A1.sb("BW", [128, 2, 16, 2048], BF16)
    XG = A1.sb("XG", [128, 2, 16, 512], BF16)
    ST = A1.sb("ST", [128, 4, 512], BF16)
    stc = [0]

    def load_w(src, ws):
        for dc in range(16):
            P.dma("pool", C("dma_start", out=BWa[:, ws, dc, :], in_=src[dc * 128:(dc + 1) * 128, :]), writes=[f"BW{ws}"], chan=f"BW{ws}")

    def load_x(src, g, slot):
        for dc in range(16):
            P.dma("pool", C("dma_start", out=XG[:, slot, dc, :], in_=src[dc * 128:(dc + 1) * 128, g * 512:(g + 1) * 512]),
                  writes=[f"XG{slot}"], chan=f"XG{slot}")

    def featmajor_pass(xsrc, ngroups, dst, scale, ws):
        for g in range(ngroups):
            slot = g % 2
            load_x(xsrc, g, slot)
            for fc in range(16):
                ps = pb[fc % 4]
                for dc in range(16):
                    P.op("pe", C("matmul", ps[:], lhsT=BWa[:, ws, dc, fc * 128:(fc + 1) * 128], rhs=XG[:, slot, dc, :], start=(dc == 0), stop=(dc == 15)),
                         reads=[f"BW{ws}", f"XG{slot}"], writes=[f"pb{fc % 4}"])
                s = stc[0] % 4; stc[0] += 1
                evac(ST[:, s, :], ps[:], [f"pb{fc % 4}"], [f"ST{s}"], scale=scale)
                P.dma("sp", C("dma_start", out=dst[fc, :, g * 512:(g + 1) * 512], in_=ST[:, s, :]), reads=[f"ST{s}"], writes=["scr1"], chan=f"ST{s}_st")

    load_w(wk, 0)
    load_w(wv, 1)
    featmajor_pass(xT_b, 16, kTs, None, 0)
    load_w(wq, 0)
    for g in range(16):
        slot = g % 2
        load_x(xT_b, g, slot)
        for tt in range(4):
            for fg in range(4):
                ps = pb[fg]
                for dc in range(16):
                    P.op("pe", C("matmul", ps[:], lhsT=XG[:, slot, dc, tt * 128:(tt + 1) * 128], rhs=BWa[:, 1, dc, fg * 512:(fg + 1) * 512], start=(dc == 0), stop=(dc == 15)),
                         reads=["BW1", f"XG{slot}"], writes=[f"pb{fg}"])
                s = stc[0] % 4; stc[0] += 1
                evac(ST[:, s, :], ps[:], [f"pb{fg}"], [f"ST{s}"])
                tile_i = g * 4 + tt
                P.dma("sp", C("dma_start", out=vS[fg * 4:(fg + 1) * 4, :, tile_i, :].rearrange("h p d -> p h d"), in_=ST[:, s, :].rearrange("p (h d) -> p h d", h=4)),
                      reads=[f"ST{s}"], writes=["scr1"], chan=f"ST{s}_st")
    featmajor_pass(xT_own, 4, qTs, 128.0 ** -0.5, 0)
    P.barrier()
    A1.close()

    A2 = Alloc(nc)
    KTa = A2.sb("KT", [128, 2, S], BF16)
    VHa = A2.sb("VH", [128, 2, 64, 128], BF16)
    QHa = A2.sb("QH", [128, 2, NT], BF16)
    SPall = A2.sb("SPall", [128, 64, 512], BF16)
    Wb = A2.sb("Wb", [128, 6, 512], BF16)
    Cs32 = A2.sb("Cs32", [128, 512], F32)
    CsB = A2.sb("CsB", [128, 2, 512], BF16)
    OT = A2.sb("OT", [128, NT], BF16)
    zer = A2.sb("zer", [128, 512], BF16)
    Hk = A2.sb("Hk", [128, LBV], F32)
    EBs = A2.sb("EBs", [128, 2048], BF16)
    kmf = A2.sb("kmf", [128, 32], F32)
    kmb = A2.sb("kmb", [128, 32], BF16)
    gm = A2.sb("gm", [128, 4, 32], F32)
    g8 = A2.sb("g8", [128, 4, 8], F32)
    t1 = A2.sb("t1", [128, 4, 32], F32)
    selb = A2.sb("selb", [128, 4, 32], BF16)
    selT = A2.sb("selT", [32, 512], BF16)
    rl = A2.sb("rl", [128, 512], F32)
    rbT = A2.sb("rbT", [32, 8], F32)
    ohx = A2.sb("ohx", [32, LBV], F32)
    bvx = Hk[0:8, :]
    esel = A2.sb("esel", [32, 32, 128], BF16)
    masks = A2.sb("masks", [128, 16, 256], BF16)
    blkm = A2.sb("blkm", [128, 3, 16, 32], F32)
    P.dma("pool", C("dma_start", out=esel[:].rearrange("p a b -> p (a b)"), in_=esel_d), writes=["esel"])
    for mm in range(8):
        P.op("dve", C("tensor_scalar", out=masks[:, mm, :], in0=qposB[:], scalar1=misc[:, 32 + mm:33 + mm], scalar2=NEGB, op0=ALU.is_le, op1=ALU.mult),
             reads=["qposB", "misc"], writes=["masks"])
        P.op("dve", C("tensor_scalar", out=masks[:, 8 + mm, :], in0=qposB[:], scalar1=misc[:, 32 + mm:33 + mm], scalar2=NEGB, op0=ALU.is_lt, op1=ALU.mult),
             reads=["qposB", "misc"], writes=["masks"])
    for tt in range(16):
        for j, (opc, val) in enumerate([(ALU.is_lt, NEGB), (ALU.is_gt, NEGB), (ALU.is_ge, -1e30)]):
            P.op("dve", C("tensor_scalar", out=blkm[:, j, tt, :], in0=iota_n, scalar1=qblkc[:, tt:tt + 1], scalar2=val, op0=opc, op1=ALU.mult),
                 reads=["misc", "qblkc"], writes=["blkm"])


    P.dma("sp", C("dma_start", out=rbT[:], in_=rbT_d), writes=["rbT"])
    P.dma("sp", C("dma_start", out=ohx[:], in_=ohx_d), writes=["ohx"])
    for c in range(5):
        n0 = c * 512; n1 = min(LBV, n0 + 512)
        P.op("pe", C("matmul", pb[c][0:8, 0:n1 - n0], lhsT=rbT[:], rhs=ohx[:, n0:n1], start=True, stop=True), reads=["rbT", "ohx"], writes=[f"pb{c}"])
        P.op("dve", C("tensor_copy", out=bvx[:, n0:n1], in_=pb[c][0:8, 0:n1 - n0]), reads=[f"pb{c}"], writes=["Hk"])
    P.dma("sp", C("dma_start", out=bvs, in_=bvx), reads=["Hk"], writes=["bvs"])

    def load_head(hh):
        hs = hh % 2
        P.dma("sp", C("dma_start", out=KTa[:, hs, :], in_=kTs[hh]), writes=[f"KT{hs}"])
        P.dma("sp", C("dma_start", out=VHa[:, hs, :, :].rearrange("p t d -> p (t d)"), in_=vS[hh].rearrange("p t d -> p (t d)")), writes=[f"VH{hs}"])
        P.dma("sp", C("dma_start", out=QHa[:, hs, :], in_=qTs[hh]), writes=[f"QH{hs}"])

    def store_head(hh):
        P.dma("sp", C("dma_start", out=oTs[hh], in_=OT[:]), reads=["OT"], writes=["oTs"], chan="OT_st")
        if debug:
            P.dma("sp", C("dma_start", out=dbg_o[hh], in_=OT[:]), reads=["OT"], writes=["dbg_o"], chan="OT_st2")

    P.op("dve", C("memset", zer[:], 0.0), writes=["zer"])
    def sb_pair(h, kp):
        hs = h % 2
        KT = KTa[:, hs, :]; VH = VHa[:, hs, :, :]; QH = QHa[:, hs, :]
        kKT = f"KT{hs}"; kVH = f"VH{hs}"; kQH = f"QH{hs}"
        k0 = 2 * kp; k1 = k0 + 1
        q = QH[:, k0 * 256:k0 * 256 + 512]
        pob = 6 + (kp % 2)
        po = pb[pob]
        pokey = f"pb{pob}"
        tiles = list(range(8 * k1 + 7, -1, -1))
        nt_ = len(tiles)
        P.op("pool", C("memset", Cs32[:], 0.0), writes=["Cs32"])
        for cs in range(2):
            P.op("dve", C("memset", CsB[:, cs, :], 0.0), writes=[f"CsB{cs}"])
        P.op("pe", C("matmul", po[:], lhsT=VH[:, 0, :], rhs=zer[:], start=True, stop=False), reads=[kVH, "zer"], writes=[pokey])

        def rng(m):
            return 256 if m >= 8 * k1 else 0

        def qk(i, m, zb, more):
            lo = rng(m)
            z = pb[zb]
            diag = m >= 8 * k0
            P.op("pe", C("matmul", z[:, lo:512], lhsT=KT[:, m * 128:(m + 1) * 128], rhs=q[:, lo:512], start=True, stop=not (diag or more)), reads=[kKT, kQH], writes=[f"pb{zb}"])
            if m >= 8 * k1:
                P.op("pe", C("matmul", z[:, 256:512], lhsT=ident, rhs=masks[:, m - 8 * k1, :], start=False, stop=not more), reads=["cm", "masks"], writes=[f"pb{zb}"])
            elif m >= 8 * k0:
                P.op("pe", C("matmul", z[:, 0:256], lhsT=ident, rhs=masks[:, m - 8 * k0, :], start=False, stop=not more), reads=["cm", "masks"], writes=[f"pb{zb}"])

        for i, m in enumerate(tiles):
            zb = i % 5; lo = rng(m)
            qk(i, m, zb, False)
            P.op("act", C("activation", out=SPall[:, i, lo:512], in_=pb[zb][:, lo:512], func=AF.Softplus), reads=[f"pb{zb}"], writes=[f"SPall{i % 8}"])

        def p2a(i, m):
            zb = i % 5; s_ = i % 6; lo = rng(m); cs = i % 2
            z = pb[zb]
            qk(i, m, zb, True)
            P.op("pe", C("matmul", z[:, lo:512], lhsT=negtri, rhs=SPall[:, i, lo:512], start=False, stop=(i == 0)), reads=["cm", f"SPall{i % 8}"], writes=[f"pb{zb}"])
            if i > 0:
                P.op("pe", C("matmul", z[:, lo:512], lhsT=negones, rhs=CsB[:, cs, lo:512], start=False, stop=True), reads=["cm", f"CsB{cs}"], writes=[f"pb{zb}"])
            P.op("act", C("activation", out=Wb[:, s_, lo:512], in_=z[:, lo:512], func=AF.Exp), reads=[f"pb{zb}"], writes=[f"Wb{s_}"])
            if i < nt_ - 1:
                P.op("pool", C("tensor_tensor", out=Cs32[:, lo:512], in0=Cs32[:, lo:512], in1=SPall[:, i, lo:512], op=ALU.add), reads=[f"SPall{i % 8}", "Cs32"], writes=["Cs32"])
                P.op("dve", C("tensor_copy", out=CsB[:, 1 - cs, :], in_=Cs32[:]), reads=["Cs32"], writes=[f"CsB{1 - cs}"])

        def p2b(i, m):
            s_ = i % 6; lo = rng(m)
            P.op("pe", C("matmul", po[:, lo:512], lhsT=VH[:, m, :], rhs=Wb[:, s_, lo:512], start=False, stop=(i == nt_ - 1)), reads=[kVH, f"Wb{s_}"], writes=[pokey])

        for st in range(nt_ + 2):
            if st < nt_:
                p2a(st, tiles[st])
            if 0 <= st - 2 < nt_:
                p2b(st - 2, tiles[st - 2])
        evac(OT[:, k0 * 256:k0 * 256 + 512], po[:], [pokey], ["OT"])

    load_head(0)
    for h in range(8):
        load_head(h + 1)
        for kp in range(4):
            sb_pair(h, kp)
        store_head(h)

    def moba_pair(h, kp):
        hs = (8 + h) % 2
        KT = KTa[:, hs, :]; VH = VHa[:, hs, :, :]; QH = QHa[:, hs, :]
        kKT = f"KT{hs}"; kVH = f"VH{hs}"; kQH = f"QH{hs}"
        k0 = 2 * kp; k1 = k0 + 1
        q = QH[:, k0 * 256:k0 * 256 + 512]
        pg = pb[5]
        for tt in range(4):
            ti = 2 * k0 + tt
            P.op("pe", C("matmul", pg[:, tt * 32:(tt + 1) * 32], lhsT=QH[:, k0 * 256 + tt * 128:k0 * 256 + (tt + 1) * 128], rhs=kmb[:], start=True, stop=True),
                 reads=[kQH, "kmb"], writes=["pb5"])
            P.op("dve", C("tensor_tensor", out=gm[:, tt, :], in0=pg[:, tt * 32:(tt + 1) * 32], in1=blkm[:, 2, ti, :], op=ALU.add), reads=["pb5", "blkm"], writes=["gm"])
            P.op("dve", C("max", out=g8[:, tt, :], in_=gm[:, tt, :]), reads=["gm"], writes=["g8"])
            P.op("dve", C("tensor_scalar", out=t1[:, tt, :], in0=gm[:, tt, :], scalar1=g8[:, tt, 2:3], scalar2=None, op0=ALU.is_lt), reads=["gm", "g8"], writes=["t1"])
            P.op("dve", C("tensor_tensor", out=t1[:, tt, :], in0=t1[:, tt, :], in1=blkm[:, 0, ti, :], op=ALU.mult), reads=["t1", "blkm"], writes=["t1"])
            P.op("dve", C("tensor_tensor", out=selb[:, tt, :], in0=t1[:, tt, :], in1=blkm[:, 1, ti, :], op=ALU.add), reads=["t1", "blkm"], writes=["selb"])
        pgT = pb[5][0:32, 256:512].bitcast(BF16)
        for tt in range(4):
            P.op("pe", C("transpose", pgT[:, tt * 128:(tt + 1) * 128], selb[:, tt, :], ident), reads=["selb", "cm"], writes=["pb5"])
        P.op("dve", C("tensor_copy", out=selT[:], in_=pgT), reads=["pb5"], writes=["selT"])
        po = pb[6]; pl = pb[7]
        ntile = 8 * k1 + 8

        def rng(m):
            return 256 if m >= 8 * k1 else 0

        def mA(m):
            zb = m % 5; s_ = m % 6; lo = rng(m)
            z = pb[zb]
            P.op("pe", C("matmul", z[:, lo:512], lhsT=KT[:, m * 128:(m + 1) * 128], rhs=q[:, lo:512], start=True, stop=False), reads=[kKT, kQH], writes=[f"pb{zb}"])
            P.op("pe", C("matmul", z[:, lo:512], lhsT=esel[:, m // 2, :], rhs=selT[:, lo:512], start=False, stop=(m < 8 * k0)), reads=["esel", "selT"], writes=[f"pb{zb}"])
            if m >= 8 * k1:
                P.op("pe", C("matmul", z[:, 256:512], lhsT=ident, rhs=masks[:, 8 + m - 8 * k1, :], start=False, stop=True), reads=["cm", "masks"], writes=[f"pb{zb}"])
            elif m >= 8 * k0:
                P.op("pe", C("matmul", z[:, 0:256], lhsT=ident, rhs=masks[:, 8 + m - 8 * k0, :], start=False, stop=True), reads=["cm", "masks"], writes=[f"pb{zb}"])
            P.op("act", C("activation", out=Wb[:, s_, lo:512], in_=z[:, lo:512], func=AF.Exp, bias=rb31c[:, h:h + 1]), reads=[f"pb{zb}", "rb31c"], writes=[f"Wb{s_}"])
            for half, kk in ((0, k0), (1, k1)):
                mm15 = m - 8 * kk + 7
                if 0 <= mm15 <= 14:
                    c0 = half * 256
                    P.op("dve", C("tensor_tensor", out=Wb[:, s_, c0:c0 + 256], in0=Wb[:, s_, c0:c0 + 256], in1=EBs[:, 128 * mm15:128 * mm15 + 256], op=ALU.mult), reads=[f"Wb{s_}", "EBs"], writes=[f"Wb{s_}"])

        def mB(m):
            s_ = m % 6; lo = rng(m)
            P.op("pe", C("matmul", po[:, lo:512], lhsT=VH[:, m, :], rhs=Wb[:, s_, lo:512], start=(m == 0), stop=(m == ntile - 1)), reads=[kVH, f"Wb{s_}"], writes=["pb6"])
            P.op("pe", C("matmul", pl[:, lo:512], lhsT=ones, rhs=Wb[:, s_, lo:512], start=(m == 0), stop=(m == ntile - 1)), reads=["cm", f"Wb{s_}"], writes=["pb7"])

        for m in range(ntile + 2):
            if m < ntile:
                mA(m)
            if 0 <= m - 2 < ntile:
                mB(m - 2)
        P.op("dve", C("reciprocal", out=rl[:], in_=pl[:]), reads=["pb7"], writes=["rl"])
        P.op("dve", C("tensor_tensor", out=OT[:, k0 * 256:k0 * 256 + 512], in0=po[:], in1=rl[:], op=ALU.mult), reads=["pb6", "rl"], writes=["OT"])

    for h in range(8):
        hh = 8 + h
        if hh + 1 < 16:
            load_head(hh + 1)
        KT = KTa[:, hh % 2, :]; kKT = f"KT{hh % 2}"
        P.dma("sp", C("dma_start", out=Hk[:, 0:2048], in_=AP(tensor=bvs.tensor, offset=h * LBV, ap=[[1, 128], [1, 2048]])), reads=["bvs"], writes=["Hk"])
        P.op("act", C("activation", out=EBs[:], in_=Hk[:, 0:2048], func=AF.Exp, bias=nrb31c[:, h:h + 1]), reads=["Hk", "nrb31c"], writes=["EBs"])
        P.op("dve", C("tensor_reduce", out=kmf[:], in_=KT[:].rearrange("p (n s) -> p n s", s=256), axis=AX.X, op=ALU.add), reads=[kKT], writes=["kmf"])
        P.op("dve", C("tensor_copy", out=kmb[:], in_=kmf[:]), reads=["kmf"], writes=["kmb"])
        for kp in range(4):
            moba_pair(h, kp)
        store_head(hh)
    P.barrier()
    A2.close()

    A3 = Alloc(nc)
    BW = A3.sb("BW3", [128, 16, 2048], BF16)
    XG3 = A3.sb("XG3", [128, 16, 512], BF16)
    OG = A3.sb("OG", [128, 16, 512], BF16)
    MT = A3.sb("MT", [128, 16, 512], BF16)
    WG = A3.sb("WG", [128, 2, 2, 16, 128], BF16)
    WB = A3.sb("WB", [128, 2, 2, 8, 128], BF16)
    GS = A3.sb("GS", [128, 2, 512], F32)
    YS = A3.sb("YS", [128, 512], F32)
    XO = A3.sb("XO", [128, D], F32)
    H1 = A3.sb("H1", [128, D], F32)
    X1B = A3.sb("X1B", [128, D], BF16)
    XTS = A3.sb("XTS", [128, 16, 128], BF16)
    LNP = A3.sb("LNP", [128, 2, D], F32)
    st6 = A3.sb("st6", [128, 4, 6], F32)
    mv = A3.sb("mv", [128, 4], F32)

    def layernorm(src_key, src, dst_key, dst):
        for c in range(4):
            P.op("dve", C("bn_stats", out=st6[:, c, :], in_=src[:, c * 512:(c + 1) * 512]), reads=[src_key], writes=["st6"])
        P.op("dve", C("bn_aggr", out=mv[:, 0:2], in_=st6[:].rearrange("p a b -> p (a b)")), reads=["st6"], writes=["mv"])
        P.op("act", C("activation", out=mv[:, 2:3], in_=mv[:, 1:2], func=AF.Sqrt, bias=EPS), reads=["mv"], writes=["mv"])
        P.op("dve", C("reciprocal", out=mv[:, 3:4], in_=mv[:, 2:3]), reads=["mv"], writes=["mv"])
        P.op("dve", C("tensor_scalar", out=dst, in0=src, scalar1=mv[:, 0:1], scalar2=mv[:, 3:4], op0=ALU.subtract, op1=ALU.mult), reads=[src_key, "mv"], writes=[dst_key])
        P.op("dve", C("tensor_tensor", out=dst, in0=dst, in1=LNP[:, 0, :], op=ALU.mult), reads=[dst_key, "LNP"], writes=[dst_key])
        P.op("dve", C("tensor_tensor", out=dst, in0=dst, in1=LNP[:, 1, :], op=ALU.add), reads=[dst_key, "LNP"], writes=[dst_key])

    for dc in range(16):
        P.dma("pool", C("dma_start", out=BW[:, dc, :], in_=w_out[dc * 128:(dc + 1) * 128, :]), writes=["BW3"], chan="BW3")
    for j in range(2):
        P.dma("sp", C("dma_start", out=LNP[:, j, :], in_=AP(tensor=ln_d.tensor, offset=j * D, ap=[[0, 128], [1, D]])), writes=["LNP"], chan="LNP")
    for g in range(4):
        for dc in range(16):
            P.dma("pool", C("dma_start", out=XG3[:, dc, :], in_=xT_own[dc * 128:(dc + 1) * 128, g * 512:(g + 1) * 512]), writes=["XG3"], chan="XG3")
        P.dma("sp", C("dma_start", out=OG[:], in_=oTs[:, :, g * 512:(g + 1) * 512].rearrange("h p t -> p h t")), writes=["OG"])
        for fc in range(16):
            sl = fc % 2
            for j in range(2):
                P.dma("pool", C("dma_start", out=WG[:, sl, j, :, :], in_=w_gate[:, j * 2048 + fc * 128:j * 2048 + (fc + 1) * 128].rearrange("(c p) f -> p c f", p=128)),
                      writes=[f"WG{sl}"], chan=f"WG{sl}")
                wsrc = wbs if j == 0 else wbm
                P.dma("pool", C("dma_start", out=WB[:, sl, j, :, :], in_=wsrc[:, fc * 128:(fc + 1) * 128].rearrange("(c p) f -> p c f", p=128)),
                      writes=[f"WB{sl}"], chan=f"WB{sl}")
            for j in range(2):
                pgt = pb[j]; pyt = pb[2 + j]
                for dc in range(16):
                    P.op("pe", C("matmul", pgt[:], lhsT=WG[:, sl, j, dc, :], rhs=XG3[:, dc, :], start=(dc == 0), stop=(dc == 15)), reads=[f"WG{sl}", "XG3"], writes=[f"pb{j}"])
                P.op("act", C("activation", out=GS[:, j, :], in_=pgt[:], func=AF.Sigmoid, bias=bgc[:, j * 16 + fc:j * 16 + fc + 1]), reads=[f"pb{j}", "bgc"], writes=[f"GS{j}"])
                for hc in range(8):
                    P.op("pe", C("matmul", pyt[:], lhsT=WB[:, sl, j, hc, :], rhs=OG[:, j * 8 + hc, :], start=(hc == 0), stop=(hc == 7)), reads=[f"WB{sl}", "OG"], writes=[f"pb{2 + j}"])
            P.op("dve", C("tensor_tensor", out=YS[:], in0=pb[2][:], in1=GS[:, 0, :], op=ALU.mult), reads=["pb2", "GS0"], writes=["YS"])
            P.op("dve", C("tensor_tensor", out=GS[:, 1, :], in0=pb[3][:], in1=GS[:, 1, :], op=ALU.mult), reads=["pb3", "GS1"], writes=["GS1"])
            P.op("dve", C("tensor_tensor", out=MT[:, fc, :], in0=YS[:], in1=GS[:, 1, :], op=ALU.add), reads=["YS", "GS1"], writes=["MT"])
        for tt in range(4):
            ti = g * 4 + tt
            P.dma("sp", C("dma_start", out=XO[:], in_=x_own[ti * 128:(ti + 1) * 128, :]), writes=["XO"])
            for fg in range(4):
                ps = pb[4 + fg]
                for fc in range(16):
                    P.op("pe", C("matmul", ps[:], lhsT=MT[:, fc, tt * 128:(tt + 1) * 128], rhs=BW[:, fc, fg * 512:(fg + 1) * 512], start=(fc == 0), stop=(fc == 15)),
                         reads=["MT", "BW3"], writes=[f"pb{4 + fg}"])
                P.op("dve", C("scalar_tensor_tensor", out=H1[:, fg * 512:(fg + 1) * 512], in0=XO[:, fg * 512:(fg + 1) * 512], scalar=ALPHA, in1=ps[:], op0=ALU.mult, op1=ALU.add),
                     reads=["XO", f"pb{4 + fg}"], writes=["H1"])
            layernorm("H1", H1[:], "H1", H1[:])
            P.dma("sp", C("dma_start", out=x1S[ti], in_=H1[:]), reads=["H1"], writes=["x1S"], chan="H1_st")
            if debug:
                P.dma("sp", C("dma_start", out=dbg_x1[ti * 128:(ti + 1) * 128, :], in_=H1[:]), reads=["H1"], writes=["dbg_x1"], chan="H1_st2")
            P.op("act", C("activation", out=X1B[:], in_=H1[:], func=AF.Copy), reads=["H1"], writes=["X1B"])
            for dcg in range(4):
                ptv = pb[dcg % 2][:, 0:256].bitcast(BF16)
                for j in range(4):
                    dc = dcg * 4 + j
                    P.op("pe", C("transpose", ptv[:, j * 128:(j + 1) * 128], X1B[:, dc * 128:(dc + 1) * 128], ident), reads=["X1B", "cm"], writes=[f"pb{dcg % 2}"])
                evac(XTS[:, dcg * 4:(dcg + 1) * 4, :].rearrange("p a b -> p (a b)"), ptv, [f"pb{dcg % 2}"], ["XTS"])
            P.dma("sp", C("dma_start", out=x1Ts[ti], in_=XTS[:]), reads=["XTS"], writes=["x1Ts"], chan="XTS_st")
    P.barrier()
    A3.close()

    A4 = Alloc(nc)
    X1T = A4.sb("X1T", [128, 2, 16, 128], BF16)
    Wk = A4.sb("Wk", [128, 16, 2048], BF16)
    Sc = A4.sb("Sc", [128, 2, 16, 128], F32)
    tops = A4.sb("tops", [128, 16, 16], F32)
    idxu = A4.sb("idxu", [128, 16, 16], U32)
    idxf = A4.sb("idxf", [128, 16, 16], F32)
    cand = A4.sb("cand", [128, 256], F32)
    cjk = A4.sb("cjk", [128, 256], F32)
    best = A4.sb("best", [128, 8, 16], F32)
    best2 = A4.sb("best2", [128, 8, 16], F32)
    posu = A4.sb("posu", [128, 128], U32)
    posf = A4.sb("posf", [128, 128], F32)
    pti = A4.sb("pti", [128, 128], mybir.dt.int32)
    paf = A4.sb("paf", [128, 128], F32)
    pr = A4.sb("pr", [128, 128], F32)
    pneg = A4.sb("pneg", [128, 128], F32)
    pbf = A4.sb("pbf", [128, 128], F32)
    e4 = A4.sb("e4", [128, 128, 16], BF16)
    nb0 = A4.sb("nb0", [128, 8], F32)
    gsum = A4.sb("gsum", [128, 8], F32)
    gw = A4.sb("gw", [128, 8, 16], F32)
    exi = A4.sb("exi", [128, 128], F32)
    exj = A4.sb("exj", [128, 128], F32)
    ijgT = A4.sb("ijgT", [128, 3, 128], F32)
    iotaF = misc[:, 48:176]
    iota16 = misc[:, 0:16]

    def acopy(out, in_, reads, writes):
        P.op("act", C("activation", out=out, in_=in_, func=AF.Copy), reads=reads, writes=writes)

    A4p = Alloc(nc)
    WT = A4p.sb("WT", [128, 1, D], F32)
    KYF = A4p.sb("KYF", [128, 256], F32)
    P.dma("sp", C("dma_start", out=KYF[:], in_=keysT_d), writes=["KYF"])
    for c in range(16):
        sl = 0
        P.dma("sp", C("dma_start", out=WT[:, sl, :], in_=wpqT[c * 128:(c + 1) * 128, :]), writes=[f"WT{sl}"])
        for d4 in range(4):
            bk = (c * 4 + d4) % 4
            for q_ in range(4):
                dc = d4 * 4 + q_
                P.op("pe", C("matmul", pb[bk][:, q_ * 128:(q_ + 1) * 128], lhsT=WT[:, sl, dc * 128:(dc + 1) * 128], rhs=KYF[:, (c % 2) * 128:(c % 2 + 1) * 128], start=True, stop=True),
                     reads=[f"WT{sl}", "KYF"], writes=[f"pb{bk}"])
            acopy(Wk[:, d4 * 4:(d4 + 1) * 4, c * 128:(c + 1) * 128], pb[bk][:].rearrange("p (a b) -> p a b", a=4), [f"pb{bk}"], ["Wk"])

    P.barrier()
    A4p.close()
    A4b = Alloc(nc)
    OHI = A4b.sb("OHI", [128, 2, 64, 128], BF16)
    OHJ = A4b.sb("OHJ", [128, 2, 64, 128], BF16)
    W3 = A4b.sb("W3", [128, 2, 128, 64], BF16)
    cand3 = cand[:].rearrange("p (a b) -> p a b", a=16)

    def p4a_S(ti):
        sl = ti % 2
        P.dma("sp", C("dma_start", out=X1T[:, sl, :, :], in_=x1Ts[ti]), writes=[f"X1T{sl}"])
        for fg in range(4):
            bk = 4 + (fg % 2)
            for dc in range(16):
                P.op("pe", C("matmul", pb[bk][:], lhsT=X1T[:, sl, dc, :], rhs=Wk[:, dc, fg * 512:(fg + 1) * 512], start=(dc == 0), stop=(dc == 15)), reads=[f"X1T{sl}", "Wk"], writes=[f"pb{bk}"])
            acopy(Sc[:, sl, fg * 4:(fg + 1) * 4, :].rearrange("p a b -> p (a b)"), pb[bk][:], [f"pb{bk}"], [f"Sc{sl}"])

    def p4a_X1(ti):
        sl = ti % 2
        sk = f"Sc{sl}"
        for c in range(16):
            P.op("dve", C("max", out=tops[:, c, 0:8], in_=Sc[:, sl, c, :]), reads=[sk], writes=["tops"])
            P.op("dve", C("max_index", out=idxu[:, c, 0:8], in_max=tops[:, c, 0:8], in_values=Sc[:, sl, c, :]), reads=[sk, "tops"], writes=["idxu"])
            P.op("dve", C("match_replace", out=Sc[:, sl, c, :], in_to_replace=tops[:, c, 0:8], in_values=Sc[:, sl, c, :], imm_value=-1e30), reads=[sk, "tops"], writes=[sk])
            P.op("dve", C("max", out=tops[:, c, 8:16], in_=Sc[:, sl, c, :]), reads=[sk], writes=["tops"])
            P.op("dve", C("max_index", out=idxu[:, c, 8:16], in_max=tops[:, c, 8:16], in_values=Sc[:, sl, c, :]), reads=[sk, "tops"], writes=["idxu"])
        P.op("dve", C("tensor_copy", out=idxf[:], in_=idxu[:]), reads=["idxu"], writes=["idxf"])
        for h in range(8):
            c0 = 2 * h; c1 = 2 * h + 1
            a_bc = tops[:, c0, :].rearrange("p (a o) -> p a o", o=1).broadcast_to([128, 16, 16])
            b_bc = tops[:, c1:c1 + 1, :].to_broadcast([128, 16, 16])
            P.op("dve", C("tensor_tensor", out=cand3, in0=a_bc, in1=b_bc, op=ALU.add), reads=["tops"], writes=["cand"])
            P.op("dve", C("max", out=best[:, h, 0:8], in_=cand[:]), reads=["cand"], writes=["best"])
            P.op("dve", C("match_replace", out=cjk[:], in_to_replace=best[:, h, 0:8], in_values=cand[:], imm_value=-1e30), reads=["cand", "best"], writes=["cjk"])
            P.op("dve", C("max", out=best[:, h, 8:16], in_=cjk[:]), reads=["cjk"], writes=["best"])
            P.op("dve", C("max_index", out=posu[:, h * 16:h * 16 + 8], in_max=best[:, h, 0:8], in_values=cand[:]), reads=["cand", "best"], writes=["posu"])
            P.op("dve", C("max_index", out=posu[:, h * 16 + 8:h * 16 + 16], in_max=best[:, h, 8:16], in_values=cand[:]), reads=["cand", "best"], writes=["posu"])
        P.op("dve", C("tensor_copy", out=posf[:], in_=posu[:]), reads=["posu"], writes=["posf"])
        P.op("dve", C("tensor_scalar", out=pti[:], in0=posf[:], scalar1=1.0 / 16.0, scalar2=None, op0=ALU.mult), reads=["posf"], writes=["pti"])
        P.op("dve", C("tensor_copy", out=paf[:], in_=pti[:]), reads=["pti"], writes=["paf"])
        P.op("dve", C("scalar_tensor_tensor", out=pr[:], in0=paf[:], scalar=-16.0, in1=posf[:], op0=ALU.mult, op1=ALU.add), reads=["paf", "posf"], writes=["pr"])
        P.op("dve", C("tensor_scalar", out=pneg[:], in0=pr[:], scalar1=0.0, scalar2=None, op0=ALU.is_lt), reads=["pr"], writes=["pneg"])
        P.op("dve", C("tensor_tensor", out=paf[:], in0=paf[:], in1=pneg[:], op=ALU.subtract), reads=["paf", "pneg"], writes=["paf"])
        P.op("dve", C("scalar_tensor_tensor", out=pbf[:], in0=pneg[:], scalar=16.0, in1=pr[:], op0=ALU.mult, op1=ALU.add), reads=["pneg", "pr"], writes=["pbf"])
        io16 = iota16.rearrange("p (o a) -> p o a", o=1).to_broadcast([128, 128, 16])
        for side, (src, dst, key) in enumerate([(paf, exi, "exi"), (pbf, exj, "exj")]):
            P.op("dve", C("tensor_tensor", out=e4[:], in0=io16, in1=src[:].rearrange("p (s o) -> p s o", o=1).broadcast_to([128, 128, 16]), op=ALU.is_equal), reads=["misc", "paf", "pbf"], writes=["e4"])
            for h in range(8):
                P.op("dve", C("tensor_tensor", out=e4[:, h * 16:(h + 1) * 16, :], in0=e4[:, h * 16:(h + 1) * 16, :], in1=idxf[:, 2 * h + side:2 * h + side + 1, :].to_broadcast([128, 16, 16]), op=ALU.mult),
                     reads=["e4", "idxf"], writes=["e4"])
            P.op("dve", C("tensor_reduce", out=dst[:], in_=e4[:], axis=AX.X, op=ALU.add), reads=["e4"], writes=[key])
        P.op("dve", C("tensor_scalar", out=nb0[:], in0=best[:, :, 0], scalar1=-1.0, scalar2=None, op0=ALU.mult), reads=["best"], writes=["nb0"])
        P.op("dve", C("tensor_copy", out=best2[:], in_=best[:]), reads=["best"], writes=["best2"])

    def p4a_X1b(ti):
        for h in range(8):
            P.op("act", C("activation", out=gw[:, h, :], in_=best2[:, h, :], func=AF.Exp, bias=nb0[:, h:h + 1], accum_out=gsum[:, h:h + 1]), reads=["best2", "nb0"], writes=["gw", "gsum"])
        P.op("dve", C("reciprocal", out=gsum[:], in_=gsum[:]), reads=["gsum"], writes=["gsum"])
        P.op("dve", C("tensor_tensor", out=gw[:], in0=gw[:], in1=gsum[:].rearrange("p (h o) -> p h o", o=1).broadcast_to([128, 8, 16]), op=ALU.mult), reads=["gw", "gsum"], writes=["gw"])

    def p4a_T(ti):
        pT = pb[6]
        for q_, (src, key) in enumerate([(exi[:], "exi"), (exj[:], "exj"), (gw[:].rearrange("p h k -> p (h k)"), "gw")]):
            P.op("pe", C("transpose", pT[:, q_ * 128:(q_ + 1) * 128], src, identf[:]), reads=[key, "identf"], writes=["pb6"])
        acopy(ijgT[:].rearrange("p a b -> p (a b)"), pT[:, 0:384], ["pb6"], ["ijgT"])

    def p4a_OH(ti, hf):
        t0 = hf * 64
        iota3 = iotaF.rearrange("p (o i) -> p o i", o=1).to_broadcast([128, 64, 128])
        bc = lambda row: ijgT[:, row, t0:t0 + 64].rearrange("p (t o) -> p t o", o=1).broadcast_to([128, 64, 128])
        P.op("dve", C("tensor_tensor", out=OHI[:, hf, :, :], in0=iota3, in1=bc(0), op=ALU.is_equal), reads=["misc", "ijgT"], writes=[f"OHI{hf}"])
        P.op("dve", C("tensor_tensor", out=OHJ[:, hf, :, :], in0=iota3, in1=bc(1), op=ALU.is_equal), reads=["misc", "ijgT"], writes=[f"OHJ{hf}"])
        P.op("pool", C("tensor_tensor", out=OHI[:, hf, :, :], in0=OHI[:, hf, :, :], in1=bc(2), op=ALU.mult), reads=[f"OHI{hf}", "ijgT"], writes=[f"OHI{hf}"])

    def p4a_Y(ti, hf):
        for t4 in range(16):
            bk = t4 % 4
            for tq in range(4):
                t = t4 * 4 + tq
                P.op("pe", C("matmul", pb[bk][:, tq * 128:(tq + 1) * 128], lhsT=OHI[:, hf, t, :], rhs=OHJ[:, hf, t, :], start=True, stop=True), reads=[f"OHI{hf}", f"OHJ{hf}"], writes=[f"pb{bk}"])
            acopy(W3[:, hf, :, t4 * 4:(t4 + 1) * 4], pb[bk][:].rearrange("p (t j) -> p j t", t=4), [f"pb{bk}"], [f"W3{hf}"])
        P.dma("sp", C("dma_start", out=Wd[ti, hf].rearrange("i j t -> i (j t)"), in_=W3[:, hf, :, :].rearrange("p j t -> p (j t)")), reads=[f"W3{hf}"], writes=["Wd"], chan=f"W3{hf}_st")

    p4a_S(0); p4a_X1(0); p4a_X1b(0); p4a_T(0); p4a_S(1)
    for ti in range(16):
        p4a_OH(ti, 0); p4a_OH(ti, 1)
        if ti + 1 < 16:
            p4a_X1(ti + 1)
        p4a_Y(ti, 0); p4a_Y(ti, 1)
        if ti + 1 < 16:
            p4a_X1b(ti + 1)
            p4a_T(ti + 1)
        if ti + 2 < 16:
            p4a_S(ti + 2)
    P.barrier()
    A4b.close()
    A4.close()

    A5 = Alloc(nc)
    X1TB = A5.sb("X1TB", [128, 16, 1024], BF16)
    ACCB = A5.sb("ACCB", [128, 8, D], F32)
    UB = A5.sb("UB", [128, 2, 16, 512], BF16)
    VB = A5.sb("VB", [128, 2, 4, D], BF16)
    WBk = A5.sb("WBk", [128, 8, 2, 4, 64], BF16)
    GB = A5.sb("GB", [128, 4, 1024], BF16)
    GT = A5.sb("GT", [128, 2, 512], BF16)
    VS = A5.sb("VS", [128, 1, D], F32)
    LNP = A5.sb("LNP4", [128, 2, D], F32)
    st6 = A5.sb("st64", [128, 4, 6], F32)
    mv = A5.sb("mv4", [128, 4], F32)

    def layernorm4(src_key, src, dst_key, dst):
        for c in range(4):
            P.op("dve", C("bn_stats", out=st6[:, c, :], in_=src[:, c * 512:(c + 1) * 512]), reads=[src_key], writes=["st64"])
        P.op("dve", C("bn_aggr", out=mv[:, 0:2], in_=st6[:].rearrange("p a b -> p (a b)")), reads=["st64"], writes=["mv4"])
        P.op("act", C("activation", out=mv[:, 2:3], in_=mv[:, 1:2], func=AF.Sqrt, bias=EPS), reads=["mv4"], writes=["mv4"])
        P.op("dve", C("reciprocal", out=mv[:, 3:4], in_=mv[:, 2:3]), reads=["mv4"], writes=["mv4"])
        P.op("dve", C("tensor_scalar", out=dst, in0=src, scalar1=mv[:, 0:1], scalar2=mv[:, 3:4], op0=ALU.subtract, op1=ALU.mult), reads=[src_key, "mv4"], writes=[dst_key])
        P.op("dve", C("tensor_tensor", out=dst, in0=dst, in1=LNP[:, 0, :], op=ALU.mult), reads=[dst_key, "LNP4"], writes=[dst_key])
        P.op("dve", C("tensor_tensor", out=dst, in0=dst, in1=LNP[:, 1, :], op=ALU.add), reads=[dst_key, "LNP4"], writes=[dst_key])

    for j in range(2):
        P.dma("sp", C("dma_start", out=LNP[:, j, :], in_=AP(tensor=ln_d.tensor, offset=(2 + j) * D, ap=[[0, 128], [1, D]])), writes=["LNP4"], chan="LNP4")
    hcnt = [0]
    vcnt = [0]
    for tb in range(2):
        for tt in range(8):
            ti = tb * 8 + tt
            P.dma("sp", C("dma_start", out=X1TB[:, :, tt * 128:(tt + 1) * 128], in_=x1Ts[ti]), writes=["X1TB"], chan="X1TB")
            P.dma("act", C("dma_start", out=ACCB[:, tt, :], in_=x1S[ti]), writes=[f"ACC{tt}"])
            P.op("act", C("activation", out=ACCB[:, tt, :], in_=ACCB[:, tt, :], func=AF.Copy, scale=ALPHA), reads=[f"ACC{tt}"], writes=[f"ACC{tt}"])
        for eb in range(32):
            sl = eb % 2
            j0 = eb * 4
            for dc in range(16):
                P.dma("pool", C("dma_start", out=UB[:, sl, dc, :], in_=upT[dc * 128:(dc + 1) * 128, j0 * 128:(j0 + 4) * 128]), writes=[f"UB{sl}"], chan=f"UB{sl}")
            for jj in range(4):
                vs_ = 0
                P.dma("sp", C("dma_start", out=VS[:, vs_, :], in_=vp[(j0 + jj) * 128:(j0 + jj + 1) * 128, :]), writes=[f"VS{vs_}"])
                P.op("act", C("activation", out=VB[:, sl, jj, :], in_=VS[:, vs_, :], func=AF.Copy), reads=[f"VS{vs_}"], writes=[f"VB{sl}"])
            for hf in range(2):
                P.dma("sp", C("dma_start", out=WBk[:, :, hf, :, :], in_=Wd[tb * 8:(tb + 1) * 8, hf, :, j0:j0 + 4, :].rearrange("n i j t -> i n j t")), writes=["WBk"])
            for jj in range(4):
                for half in range(2):
                    hb = hcnt[0] % 4; hcnt[0] += 1
                    ph = pb[hb]
                    for dc in range(16):
                        P.op("pe", C("matmul", ph[:], lhsT=UB[:, sl, dc, jj * 128:(jj + 1) * 128], rhs=X1TB[:, dc, half * 512:(half + 1) * 512], start=(dc == 0), stop=(dc == 15)),
                             reads=[f"UB{sl}", "X1TB"], writes=[f"pb{hb}"])
                    gs = hb % 2
                    P.op("act", C("activation", out=GT[:, gs, :], in_=ph[:], func=AF.Gelu), reads=[f"pb{hb}"], writes=[f"GT{gs}"])
                    P.op("dve", C("tensor_tensor", out=GB[:, jj, half * 512:(half + 1) * 512].rearrange("p (n h t) -> p n h t", n=4, h=2), in0=GT[:, gs, :].rearrange("p (n h t) -> p n h t", n=4, h=2),
                                  in1=WBk[:, half * 4:(half + 1) * 4, :, jj, :], op=ALU.mult), reads=[f"GT{gs}", "WBk"], writes=[f"GB{jj}"])
            for tt in range(8):
                for fg in range(4):
                    po = pb[4 + fg]
                    for jj in range(4):
                        P.op("pe", C("matmul", po[:], lhsT=GB[:, jj, tt * 128:(tt + 1) * 128], rhs=VB[:, sl, jj, fg * 512:(fg + 1) * 512], start=(jj == 0), stop=(jj == 3)),
                             reads=[f"GB{jj}", f"VB{sl}"], writes=[f"pb{4 + fg}"])
                    P.op("dve", C("tensor_tensor", out=ACCB[:, tt, fg * 512:(fg + 1) * 512], in0=ACCB[:, tt, fg * 512:(fg + 1) * 512], in1=po[:], op=ALU.add), reads=[f"ACC{tt}", f"pb{4 + fg}"], writes=[f"ACC{tt}"])
        for tt in range(8):
            ti = tb * 8 + tt
            layernorm4(f"ACC{tt}", ACCB[:, tt, :], f"ACC{tt}", ACCB[:, tt, :])
            P.dma("sp", C("dma_start", out=out_d[ti * 128:(ti + 1) * 128, :], in_=ACCB[:, tt, :]), reads=[f"ACC{tt}"], writes=["out"], chan=f"ACC{tt}_st")
    P.barrier()
    A5.close()
    P.emit()
    return nc


def _bucket(rel):
    rel = np.maximum(rel, 0)
    logd = np.log(np.maximum(rel, 1).astype(np.float32) / np.float32(16)) / np.float32(np.log(1024 / 16))
    large = 16 + (logd.astype(np.float32) * np.float32(16)).astype(np.int32)
    large = np.minimum(large, 31)
    return np.where(rel < 16, rel, large)


_NC = {}


def kernel(x, w_in, w_gate, b_gate, w_branch_sb, w_branch_moba, w_out, rel_bias, ln1_g, ln1_b,
           w_peer_query, peer_sub_keys, peer_u, peer_v, ln2_g, ln2_b, _debug=False):
    f = np.float32
    x = np.asarray(x, f)
    w_in = np.asarray(w_in, f)
    c_ = np.ascontiguousarray
    wq = c_(np.concatenate([w_in[:, 0:1024], w_in[:, 3072:4096]], axis=1))
    wk = c_(np.concatenate([w_in[:, 1024:2048], w_in[:, 4096:5120]], axis=1))
    wv = c_(np.concatenate([w_in[:, 2048:3072], w_in[:, 5120:6144]], axis=1))
    bgc = c_(np.asarray(b_gate, f).reshape(32, 128).T)
    rbT = c_(np.asarray(rel_bias, f).T)
    rb31 = c_(np.asarray(rel_bias, f)[:, 31].reshape(1, 8))
    lnp = c_(np.stack([ln1_g, ln1_b, ln2_g, ln2_b]).astype(f))
    psk = np.asarray(peer_sub_keys, f)
    keysT = c_(np.concatenate([psk[0].T, psk[1].T], axis=1))
    cmat = np.zeros((128, 512), f)
    cmat[:, 0:128] = np.eye(128)
    cmat[:, 128:256] = 1.0
    cmat[:, 256:384] = -1.0
    jj, ss = np.meshgrid(np.arange(128), np.arange(128), indexing="ij")
    cmat[:, 384:512] = -(jj >= ss).astype(f)
    esel = np.zeros((32, 32, 128), f)
    for n in range(32):
        esel[n, n, :] = 1.0
    esel = esel.reshape(32, 32 * 128)
    misc = np.zeros((128, 176), f)
    misc[:, 48:176] = np.arange(128)[None, :]
    misc[:, 0:32] = np.arange(32)[None, :]
    misc[:, 32:40] = np.arange(128)[:, None] + 128 * np.arange(8)[None, :]
    xTs = [c_(x[b].T) for b in range(2)]
    wgate = c_(np.asarray(w_gate, f)); wbs = c_(np.asarray(w_branch_sb, f)); wbm = c_(np.asarray(w_branch_moba, f))
    wo = c_(np.asarray(w_out, f)); wpqT = c_(np.asarray(w_peer_query, f).T)
    pu = c_(np.asarray(peer_u, f).reshape(128, 128, D).transpose(2, 1, 0).reshape(D, 16384))
    pv = c_(np.asarray(peer_v, f).reshape(128, 128, D).transpose(1, 0, 2).reshape(16384, D))
    in_maps = []
    toks = []
    for c in range(8):
        b, r = c // 4, c % 4
        tok = np.concatenate([256 * (4 * k + r) + np.arange(255, -1, -1) for k in range(8)])
        toks.append((b, tok))
        x_own = c_(x[b][tok])
        w = np.arange(LBV)
        ohx = (_bucket(256 * r + 1151 - w)[None, :] == np.arange(32)[:, None]).astype(f)
        qblk = (tok // 256).astype(f)
        in_maps.append(dict(
            xT_b=xTs[b], x_own=x_own, xT_own=c_(x_own.T), wq=wq, wk=wk, wv=wv, w_gate=wgate, bgc=bgc,
            wbs=wbs, wbm=wbm, w_out=wo, rbT=rbT, rb31=rb31, ohx=c_(ohx), lnp=lnp, wpqT=wpqT, keysT=keysT,
            upT=pu, vp=pv, qpos=c_(tok.astype(f).reshape(1, NT)), qblkc=c_(qblk.reshape(16, 128).T),
            cmat=cmat, esel=esel, misc=misc))
    key = bool(_debug)
    if key not in _NC:
        _NC[key] = build(debug=key)
    res = run_bass_kernel_spmd(_NC[key], in_maps, core_ids=list(range(8)))
    out = np.zeros((2, S, D), f)
    for c in range(8):
        b, tok = toks[c]
        out[b][tok] = res.results[c]["out"]
    if _debug:
        return out, res, toks
    return out
```
